# Optimizing a Trainium2 kernel written in Bass

```python
import jax, jax.numpy as jnp
from jax import lax
import numpy as np

D_MODEL = 1024
BATCH = 4
SEQ = 4096
DEPTH = 1

CHUNK = 64
HEAD_DIM = 64
FOX_HEADS = 8
FOX_WIDTH = FOX_HEADS * HEAD_DIM
SGU_GROUPS = 8
SGU_GROUP_DIM = 64
SGU_WIDTH = SGU_GROUPS * SGU_GROUP_DIM
SGU_LEN = 128
Q_BLOCK = 128
N_BRANCH = 2
D_FF = 4 * D_MODEL
IN_COLS = 3 * FOX_WIDTH + FOX_HEADS + 2 * SGU_WIDTH + N_BRANCH * D_MODEL
EPS = 1e-6

kernel_name = "hybrid_fox_gmlp_gated_block"


def rmsnorm(x, g):
    xf = x.astype(jnp.float32)
    y = xf * lax.rsqrt(jnp.mean(xf * xf, axis=-1, keepdims=True) + EPS)
    return (y * g.astype(jnp.float32)).astype(x.dtype)


def layernorm(x, g, b):
    xf = x.astype(jnp.float32)
    mu = jnp.mean(xf, axis=-1, keepdims=True)
    xc = xf - mu
    y = xc * lax.rsqrt(jnp.mean(xc * xc, axis=-1, keepdims=True) + EPS)
    return (y * g.astype(jnp.float32) + b.astype(jnp.float32)).astype(x.dtype)


def forgetting_attention(q, k, v, logf):
    b, s, h, dh = q.shape
    c = jnp.cumsum(logf, axis=1)
    c_bh = jnp.transpose(c, (0, 2, 1))
    scale = dh ** -0.5
    outs = []
    for i in range(s // Q_BLOCK):
        s0 = i * Q_BLOCK
        e = s0 + Q_BLOCK
        logits = jnp.einsum('bqhd,bkhd->bhqk', q[:, s0:e], k[:, :e]).astype(jnp.float32) * scale
        logits = logits + c_bh[:, :, s0:e, None] - c_bh[:, :, None, :e]
        qpos = s0 + jnp.arange(Q_BLOCK)
        mask = qpos[:, None] >= jnp.arange(e)[None, :]
        logits = jnp.where(mask[None, None], logits, -jnp.inf)
        p = jax.nn.softmax(logits, axis=-1)
        outs.append(jnp.einsum('bhqk,bkhd->bqhd', p.astype(v.dtype), v[:, :e]))
    return jnp.concatenate(outs, axis=1)


def spatial_gating(u, v, ln_g, ln_b, w_s, b_s):
    b, s, _ = v.shape
    v = layernorm(v, ln_g, ln_b)
    vc = v.reshape(b, s // SGU_LEN, SGU_LEN, SGU_GROUPS, SGU_GROUP_DIM)
    chunk_id = jnp.arange(SGU_LEN) // CHUNK
    mask = chunk_id[None, :] <= chunk_id[:, None]
    w = jnp.where(mask[None], w_s, 0)
    mixed = jnp.einsum('gij,bnjgc->bnigc', w, vc) + jnp.transpose(b_s)[None, None, :, :, None]
    return u * mixed.reshape(b, s, SGU_WIDTH)


def setup_inputs(seed: int = 0) -> dict:
    key = jax.random.key(seed)
    ks = jax.random.split(key, 16)
    f32 = jnp.float32
    nrm = lambda k, shape, fan_in: jax.random.normal(k, shape, f32) * (fan_in ** -0.5)
    return {
        "x": jax.random.normal(ks[0], (BATCH, SEQ, D_MODEL), f32),
        "norm1_g": 1.0 + 0.02 * jax.random.normal(ks[1], (DEPTH, D_MODEL), f32),
        "w_in": nrm(ks[2], (DEPTH, D_MODEL, IN_COLS), D_MODEL),
        "b_f": jax.random.uniform(ks[3], (DEPTH, FOX_HEADS), f32, 1.0, 3.0),
        "ln_v_g": 1.0 + 0.02 * jax.random.normal(ks[4], (DEPTH, SGU_WIDTH), f32),
        "ln_v_b": 0.02 * jax.random.normal(ks[5], (DEPTH, SGU_WIDTH), f32),
        "w_sgu": nrm(ks[6], (DEPTH, SGU_GROUPS, SGU_LEN, SGU_LEN), SGU_LEN),
        "b_sgu": 1.0 + 0.1 * jax.random.normal(ks[7], (DEPTH, SGU_GROUPS, SGU_LEN), f32),
        "w_a": nrm(ks[8], (DEPTH, FOX_WIDTH, D_MODEL), FOX_WIDTH),
        "w_b": nrm(ks[9], (DEPTH, SGU_WIDTH, D_MODEL), SGU_WIDTH),
        "w_o": nrm(ks[10], (DEPTH, D_MODEL, D_MODEL), D_MODEL),
        "norm2_g": 1.0 + 0.02 * jax.random.normal(ks[11], (DEPTH, D_MODEL), f32),
        "w_up": nrm(ks[12], (DEPTH, D_MODEL, D_FF), D_MODEL),
        "w_down": nrm(ks[13], (DEPTH, D_FF, D_MODEL), D_FF),
        "normf_g": 1.0 + 0.02 * jax.random.normal(ks[14], (D_MODEL,), f32),
    }


def reference(x, norm1_g, w_in, b_f, ln_v_g, ln_v_b, w_sgu, b_sgu, w_a, w_b, w_o,
              norm2_g, w_up, w_down, normf_g):
    bsz, seq, _ = x.shape
    splits = np.cumsum([FOX_WIDTH, FOX_WIDTH, FOX_WIDTH, FOX_HEADS, SGU_WIDTH, SGU_WIDTH])
    for l in range(DEPTH):
        h = rmsnorm(x, norm1_g[l])
        p = h @ w_in[l]
        q, k, v, f_logit, u, sv, gates = jnp.split(p, splits, axis=-1)
        heads = (bsz, seq, FOX_HEADS, HEAD_DIM)
        logf = jax.nn.log_sigmoid((f_logit + b_f[l]).astype(jnp.float32))
        y_a = forgetting_attention(q.reshape(heads), k.reshape(heads), v.reshape(heads), logf)
        y_a = y_a.reshape(bsz, seq, FOX_WIDTH) @ w_a[l]
        y_b = spatial_gating(jax.nn.gelu(u), jax.nn.gelu(sv), ln_v_g[l], ln_v_b[l],
                             w_sgu[l], b_sgu[l]) @ w_b[l]
        g_a, g_b = jnp.split(gates, N_BRANCH, axis=-1)
        merged = jax.nn.sigmoid(g_a) * y_a + jax.nn.sigmoid(g_b) * y_b
        x = x + merged @ w_o[l]
        h2 = rmsnorm(x, norm2_g[l])
        x = x + jnp.square(jax.nn.relu(h2 @ w_up[l])) @ w_down[l]
    return rmsnorm(x, normf_g)
```

```python
import contextlib
import numpy as np
import concourse.bass as bass
import concourse.mybir as mybir
from concourse.bass_utils import run_bass_kernel_spmd

F32 = mybir.dt.float32
BF16 = mybir.dt.bfloat16
AF = mybir.ActivationFunctionType
ALU = mybir.AluOpType

D = 1024
T = 2048
NH = 8
EPS = 1e-6
NEG = -240000.0
KMASK = -30000.0


class Buf:
    __slots__ = ("name", "w", "r", "rd", "cnt")

    def __init__(self, name):
        self.name = name
        self.w = None
        self.r = {}
        self.rd = []
        self.cnt = 0


class Op:
    __slots__ = ("eng", "fn", "deps", "sem", "val", "signal", "dma")

    def __init__(self, eng, fn, dma=False):
        self.eng = eng
        self.fn = fn
        self.deps = []
        self.sem = None
        self.val = None
        self.signal = False
        self.dma = dma


class Sched:
    ENGS = ("pe", "act", "dve", "pool", "sp")

    def __init__(self, nc):
        self.nc = nc
        self.ops = {e: [] for e in self.ENGS}

    def _add(self, o, reads, writes):
        deps = {}
        for b in reads:
            if b.w is not None:
                deps[id(b.w)] = b.w
        for b in writes:
            if b.w is not None:
                deps[id(b.w)] = b.w
            for r in b.r.values():
                deps[id(r)] = r
            for r in b.rd:
                deps[id(r)] = r
        for d in deps.values():
            if d is o:
                continue
            if d.eng == "pe" and o.eng == "pe" and not d.dma and not o.dma:
                continue
            o.deps.append(d)
            d.signal = True
        for b in reads:
            if o.dma:
                b.rd.append(o)
            else:
                b.r[o.eng] = o
        for b in writes:
            b.w = o
            b.r = {}
            b.rd = []
        self.ops[o.eng].append(o)
        return o

    def op(self, eng, fn, reads=(), writes=()):
        return self._add(Op(eng, fn), reads, writes)

    def dma(self, eng, fn, reads=(), writes=(), sbuf=None):
        o = Op(eng, fn, dma=True)
        o.sem = sbuf
        sbuf.cnt += 1
        o.val = 16 * sbuf.cnt
        return self._add(o, reads, writes)

    def reuse(self, new, olds):
        for b in olds:
            if b.w is not None:
                new.rd.append(b.w)
            new.rd.extend(b.r.values())
            new.rd.extend(b.rd)

    def emit(self, stack):
        nc = self.nc
        eng_sem = {e: stack.enter_context(nc.semaphore("s_" + e)) for e in self.ENGS}
        bufsems = {}
        for e in self.ENGS:
            c = 0
            for o in self.ops[e]:
                if o.dma:
                    b = o.sem
                    if id(b) not in bufsems:
                        bufsems[id(b)] = stack.enter_context(nc.semaphore("d_" + b.name))
                    o.sem = bufsems[id(b)]
                elif o.signal:
                    c += 1
                    o.sem = eng_sem[e]
                    o.val = c
        handles = {"pe": "tensor", "act": "scalar", "dve": "vector", "pool": "gpsimd", "sp": "sync"}
        block = stack.enter_context(nc.Block())
        for e in self.ENGS:
            ops = self.ops[e]
            if not ops:
                continue

            def body(h, ops=ops):
                waited = {}
                for o in ops:
                    for d in o.deps:
                        k = id(d.sem)
                        if waited.get(k, -1) >= d.val:
                            continue
                        h.wait_ge(d.sem, d.val)
                        waited[k] = d.val
                    if o.fn is None:
                        continue
                    ins = o.fn(h)
                    if o.dma:
                        ins.then_inc(o.sem, 16)
                    elif o.signal:
                        ins.then_inc(o.sem, 1)

            getattr(block, handles[e])(body)


KB = 1024
OFF_P = 0
OFF_T = 16 * KB
OFF_W = 28 * KB
OFF_RA = 44 * KB
OFF_RB = 76 * KB
OFF_RC = 108 * KB
OFF_RD = 140 * KB
OFF_RE = 174 * KB
ARENA = 206 * KB

C_Q, C_K, C_V, C_F, C_U, C_SV, C_GA, C_GB = 0, 512, 1024, 1536, 1544, 2056, 2568, 3592


def build(stop=None, dbg=()):
    nc = bass.Bass("TRN2", target_bir_lowering=False)
    dt_in = lambda n, s: nc.dram_tensor(n, s, F32, kind="ExternalInput").ap()
    xprev = dt_in("xprev", [D, T])
    xown = dt_in("xown", [D, T])
    kmask_d = dt_in("kmask", [128, 32])
    w_in = dt_in("w_in", [D, 4616])
    w_a = dt_in("w_a", [512, D])
    w_b = dt_in("w_b", [512, D])
    w_o = dt_in("w_o", [D, D])
    w_up = dt_in("w_up", [D, 4096])
    w_down = dt_in("w_down", [4096, D])
    gains_d = dt_in("gains", [128, 24])
    bf_d = dt_in("b_f", [1, 8])
    lnv_d = dt_in("lnv", [2, 512])
    wsT_d = dt_in("wsT", [128, 8 * 128])
    bsgu_d = dt_in("b_sgu", [8, 128])
    cst_d = dt_in("cst", [128, 7 * 128])
    bexp_d = dt_in("bexp", [32, 256])
    out_d = nc.dram_tensor("out", [D, T], F32, kind="ExternalOutput").ap()
    scr_d = nc.dram_tensor("scr", [32, 512], F32, kind="Internal").ap()
    dbg_d = {n: nc.dram_tensor("dbg_" + n, list(shp), F32, kind="ExternalOutput").ap() for n, shp in dbg}

    st = contextlib.ExitStack()
    with st:
        S = Sched(nc)
        arena = nc.alloc_sbuf_tensor("arena", [128, ARENA // 4], F32)
        base = nc.lookup_mloc(arena).addr
        cur = {}

        def at(name, shape, dt, off):
            return nc.alloc_sbuf_tensor_at(name, shape, dt, offset=base + off)

        poff = [OFF_P]

        def palloc(name, shape, dt):
            nbytes = shape[1] * (4 if dt == F32 else 2)
            nbytes = (nbytes + 31) // 32 * 32
            t = at(name, shape, dt, poff[0])
            poff[0] += nbytes
            assert poff[0] <= OFF_T
            return t

        identb = palloc("identb", [128, 128], BF16)
        trimaskb = palloc("trimaskb", [128, 128], BF16)
        onesb = palloc("onesb", [128, 128], BF16)
        trif = palloc("trif", [128, 128], F32)
        onesf = palloc("onesf", [128, 128], F32)
        gains = palloc("gains", [128, 24], F32)
        bfb = palloc("bfb", [128, 32], F32)
        lng = palloc("lng", [128, 512], F32)
        lnb = palloc("lnb", [128, 512], F32)
        wsTb = palloc("wsTb", [128, 8 * 128], BF16)
        bsT = palloc("bsT", [128, 4 * 128], F32)
        wf = palloc("wf", [128, 64], BF16)
        AK = palloc("AK", [128, 32 * 56], BF16)
        AQ = palloc("AQ", [128, 16 * 56], BF16)
        acc = palloc("acc", [128, 8], F32)
        pref2 = [palloc("pref", [128, 32], F32), at("pref1", [128, 32], F32, OFF_T + 10 * KB + 1152)]
        kmask = palloc("kmask", [128, 32], F32)
        mv16 = palloc("mv16", [128, 32], F32)
        rs16 = palloc("rs16", [128, 16], F32)
        st6 = palloc("st6", [128, 8], F32)

        rstd = at("rstd", [128, 512], F32, OFF_T)
        tvar = at("tvar", [128, 512], F32, OFF_T + 2 * KB)
        ysb = [at("ysb%d" % i, [128, 512], F32, OFF_T + (4 + 2 * i) * KB) for i in range(2)]
        rec = [at("rec0", [128, 512], F32, 204 * KB), at("rec1", [128, 512], F32, OFF_T + 8 * KB)]
        sm = OFF_T + 10 * KB
        zt = at("zt", [128, 32], F32, sm)
        et = at("et", [128, 32], F32, sm + 128)
        spt2 = [at("spt", [128, 32], F32, sm + 256), at("spt1", [128, 32], F32, sm + 1024)]
        hib = at("hib", [128, 32], BF16, sm + 384)
        lob = at("lob", [128, 32], BF16, sm + 448)
        lo2b = at("lo2b", [128, 32], BF16, sm + 512)
        r1 = at("r1", [128, 32], F32, sm + 576)
        r2 = at("r2", [128, 32], F32, sm + 704)
        tv16 = at("tv16", [128, 16], F32, sm + 832)

        W = [at("W%d" % i, [128, 4096], BF16, OFF_W + 8 * KB * i) for i in range(2)]
        Wo = at("Wo", [128, 8192], BF16, OFF_W)
        Walt = [at("Walt%d" % i, [128, 4096], BF16, OFF_RD + (18 + 8 * i) * KB) for i in range(2)]
        W23 = [at("W%d" % (2 + i), [128, 4096], BF16, OFF_RE + 8 * KB * i) for i in range(2)]

        hown = at("hown", [128, 8 * T], BF16, OFF_RA)
        hprev = at("hprev", [128, 8 * T], BF16, OFF_RC)
        xbuf = [at("xbuf%d" % i, [128, 8 * 512], F32, OFF_RB + 16 * KB * i) for i in range(2)]
        sq = at("sq", [128, 8 * 512], BF16, OFF_RE)
        VP = at("VP", [128, 32 * 516], BF16, OFF_RD)
        KpT = [at("KpT%d" % i, [128, 4096], BF16, OFF_RE + 8 * KB * i) for i in range(2)]
        QpT = [at("QpT%d" % i, [128, T], BF16, OFF_RE + 16 * KB + 4 * KB * i) for i in range(2)]
        PT = [at("PT%d" % i, [128, 1024], BF16, OFF_RE + 24 * KB + 2 * KB * i) for i in range(3)]
        yaT = at("yaT", [128, 4 * T], BF16, OFF_RB)
        uT = at("uT", [128, 4 * T], BF16, OFF_RE)
        vln = [at("vln%d" % i, [128, 512], BF16, OFF_RE + 16 * KB + KB * i) for i in range(2)]
        vtmp = [at("vtmp%d" % i, [128, 512], F32, OFF_RE + 18 * KB + 2 * KB * i) for i in range(2)]
        stmp = [at("stmp%d" % i, [128, 512], F32, OFF_RE + 22 * KB + 2 * KB * i) for i in range(2)]
        g32all = at("g32all", [128, 16 * 512], F32, OFF_RD)
        mergedT = at("mergedT", [128, 8 * T], BF16, OFF_RC)
        sig = [[at("sig%d%d" % (i, j), [128, 512], F32, OFF_RD + (4 * i + 2 * j) * KB) for j in range(2)] for i in range(2)]
        m1 = [[at("m1%d%d" % (i, j), [128, 512], F32, OFF_RD + (8 + 4 * i + 2 * j) * KB) for j in range(2)] for i in range(2)]
        xres = [at("xres%d" % i, [128, 512], F32, OFF_RE + (16 + 2 * i) * KB) for i in range(3)]
        x1T = at("x1T", [128, 8 * T], F32, OFF_RA)
        h2T = at("h2T", [128, 8 * T], BF16, OFF_RD)
        aT = [at("aT%d" % i, [128, 4 * 512], BF16, OFF_RC + 4 * KB * i) for i in range(2)]
        rl = [at("rl%d" % i, [128, 4 * 512], F32, OFF_RC + 8 * KB + 8 * KB * i) for i in range(2)]
        sq2 = at("sq2", [128, 8 * 512], BF16, OFF_RE + 22 * KB)
        ostage = at("ostage", [128, 8 * 512], F32, OFF_RC)

        FZ = OFF_RE + 24 * KB
        spt_all = at("spt_all", [128, 256], F32, FZ)
        w_all = at("w_all", [128, 256], F32, FZ + 1 * KB)
        cs_all = at("cs_all", [128, 256], F32, FZ + 2 * KB)
        r1a = at("r1a", [128, 256], F32, FZ + 3 * KB)
        hia = at("hia", [128, 256], BF16, FZ + 4 * KB)
        loa = at("loa", [128, 256], BF16, FZ + 4 * KB + 512)
        lo2a = at("lo2a", [128, 256], BF16, FZ + 5 * KB)
        xfin = at("xfin", [128, 256], F32, OFF_RE + 21 * KB)
        totT = at("totT", [128, 8], F32, OFF_RE + 23 * KB)
        bexp = at("bexp", [128, 256], F32, OFF_RE + 22 * KB)
        maskEb = at("maskEb", [128, 128], BF16, OFF_T + 2 * KB)
        maskOb = at("maskOb", [128, 128], BF16, OFF_T + 2 * KB + 256)
        negb = at("negb", [128, 128], BF16, OFF_T + 2 * KB + 512)
        pss = nc.alloc_psum_tensor("pss", [128, 8 * 512], F32)

        def ps(b, n=1):
            return pss[:, b * 512:(b + n) * 512]

        Bn = {}

        def B(name):
            if name not in Bn:
                Bn[name] = Buf(name)
            return Bn[name]

        PB = [B("psb%d" % i) for i in range(8)]

        def mm(out, lhsT, rhs, start, stop, r, w):
            S.op("pe", lambda e, a=(out, lhsT, rhs, start, stop): e.matmul(a[0], a[1], a[2], start=a[3], stop=a[4]),
                 reads=r, writes=w)

        def act(out, in_, func, r, w, scale=1.0, bias=None):
            if bias is None:
                S.op("act", lambda e, a=(out, in_, func, scale): e.activation(out=a[0], in_=a[1], func=a[2], scale=a[3]),
                     reads=r, writes=w)
            else:
                S.op("act", lambda e, a=(out, in_, func, scale, bias): e.activation(out=a[0], in_=a[1], func=a[2], scale=a[3], bias=a[4]),
                     reads=r, writes=w)

        def tt(eng, out, in0, in1, op, r, w):
            S.op(eng, lambda e, a=(out, in0, in1, op): e.tensor_tensor(a[0], a[1], a[2], a[3]), reads=r, writes=w)

        def ts(eng, out, in0, s1, s2, op0, op1, r, w):
            S.op(eng, lambda e, a=(out, in0, s1, s2, op0, op1): e.tensor_scalar(a[0], a[1], a[2], a[3], a[4], a[5]), reads=r, writes=w)

        def cp(eng, out, in_, r, w):
            S.op(eng, lambda e, a=(out, in_): e.tensor_copy(a[0], a[1]), reads=r, writes=w)

        def ld(eng, out, in_, wbuf, r=()):
            S.dma(eng, lambda e, a=(out, in_): e.dma_start(out=a[0], in_=a[1]), reads=r, writes=[wbuf], sbuf=wbuf)

        def wview(t, n):
            return t[:, 0:8 * n].rearrange("p (kc n) -> p kc n", n=n)

        def wsrc(w, c0, n, r0=0, nk=8):
            return w[r0:r0 + nk * 128, c0:c0 + n].rearrange("(kc p) n -> p kc n", p=128)

        def xload(ci, r=()):
            src = xown if ci >= 4 else xprev
            t0 = (ci % 4) * 512
            i_ = ci % 2
            dst3 = xbuf[i_][:, :].rearrange("p (kc t) -> p kc t", t=512)
            src3 = src[:, t0:t0 + 512].rearrange("(kc p) t -> p kc t", p=128)
            ba, bb = B("xbuf%d" % i_), B("xbuf%db" % i_)
            if ci < 2:
                ld("sp", dst3[:, 0:4, :], src3[:, 0:4, :], ba, r=r)
                ld("act", dst3[:, 4:8, :], src3[:, 4:8, :], bb, r=r)
            else:
                S.dma("sp", lambda e, a=(dst3, src3): e.dma_start(out=a[0], in_=a[1]), writes=[ba, bb], sbuf=ba)

        ld("sp", gains[:], gains_d[:, :], B("gains"))
        ld("sp", kmask[:], kmask_d[:, :], B("kmask"))
        xload(0)
        ld("pool", onesb[:], cst_d[:, 256:384], B("onesb"))
        ld("pool", wview(W[0], 512), wsrc(w_in, C_V, 512), B("W0"), r=[B("xbuf0"), B("xbuf0b")])
        ld("pool", wf[:, :].rearrange("p (kc n) -> p kc n", n=8), wsrc(w_in, C_F, 8), B("wf"))
        ld("sp", trif[:], cst_d[:, 128:256], B("trif"))
        ld("sp", onesf[:], cst_d[:, 256:384], B("onesf"))
        for b4 in range(4):
            ld("sp", bfb[:, b4 * 8:(b4 + 1) * 8], bf_d[0:1, :].broadcast_to([128, 8]), B("bfb%d" % b4))
        xload(1, r=[B("xbuf0"), B("xbuf0b")])
        ld("sp", lng[:], lnv_d[0:1, :].broadcast_to([128, 512]), B("lng"))
        ld("sp", lnb[:], lnv_d[1:2, :].broadcast_to([128, 512]), B("lnb"))
        for g in range(8):
            pr, gp = g // 2, g % 2
            ld("sp", bsT[gp * 64:(gp + 1) * 64, pr * 128:(pr + 1) * 128], bsgu_d[g:g + 1, :].broadcast_to([64, 128]), B("bsT%d" % g))
        ld("pool", identb[:], cst_d[:, 0:128], B("identb"))
        ld("pool", trimaskb[:], cst_d[:, 384:512], B("trimaskb"))
        ld("pool", wsTb[:], wsT_d[:, :], B("wsTb"))
        bfbR = [B("bfb%d" % i) for i in range(4)]
        bsTR = [B("bsT%d" % i) for i in range(8)]
        S.op("dve", lambda e: e.memset(AK[:], 1.0), writes=[B("AK")])
        S.op("dve", lambda e: e.memset(AQ[:], 8.0), writes=[B("AQ")])
        S.op("dve", lambda e: e.memset(acc[:], 0.0), writes=[B("acc")])
        AK4 = AK[:, :].rearrange("p (b h c) -> p b h c", h=8, c=7)
        AQ4 = AQ[:, :].rearrange("p (b h c) -> p b h c", h=8, c=7)
        for h in range(8):
            cp("dve", AK4[:, :, h, 6], kmask[:, :], [B("kmask"), B("AK")], [B("AK")])

        g1 = lambda kc: gains[:, kc:kc + 1]
        g2 = lambda kc: gains[:, 8 + kc:9 + kc]
        gf = lambda kc: gains[:, 16 + kc:17 + kc]

        def rms_chunk(xsrc_fn, sqt, sqB, gfn, dst_fn, xR, dstW, psb):
            S.op("act", lambda e, a=(sqt, xsrc_fn): e.activation(out=a[0][:, :], in_=a[1](None), func=AF.Square),
                 reads=xR, writes=[sqB])
            for kc in range(8):
                mm(ps(psb), onesb[:], sqt[:, kc * 512:(kc + 1) * 512], kc == 0, kc == 7, [B("sq%d" % kc), B("onesb")], [PB[psb]])
            ts("dve", tvar[:], ps(psb), 1.0 / D, EPS, ALU.mult, ALU.add, [PB[psb]], [B("tvar")])
            act(tvar[:], tvar[:], AF.Ln, [B("tvar")], [B("tvar")])
            act(rstd[:], tvar[:], AF.Exp, [B("tvar")], [B("rstd")], scale=-0.5)
            for kc in range(8):
                S.op("dve", lambda e, a=(dst_fn(kc), xsrc_fn(kc), gfn(kc)): e.scalar_tensor_tensor(
                    out=a[0], in0=a[1], scalar=a[2], in1=rstd[:], op0=ALU.mult, op1=ALU.mult),
                    reads=xR + [B("rstd"), B("gains")], writes=dstW)

        VP4 = VP[:, :].rearrange("p (b q c) -> p b q c", q=4, c=129)
        S.op("dve", lambda e: e.memset(VP4[:, :, :, 64], 1.0), writes=[B("VP")])

        def vslab(blk, h):
            base = blk * 516 + (h // 2) * 129
            return VP[:, base:base + 65] if h % 2 == 0 else VP[:, base + 1:base + 129]
        def rms_rest(xsrc_fn, sqt, sqB, gfn, dst_fn, xR, dstW, psb):
            for kc in range(8):
                mm(ps(psb), onesb[:], sqt[:, kc * 512:(kc + 1) * 512], kc == 0, kc == 7, [B("sq%d" % kc), B("onesb")], [PB[psb]])
            ts("dve", tvar[:], ps(psb), 1.0 / D, EPS, ALU.mult, ALU.add, [PB[psb]], [B("tvar")])
            act(tvar[:], tvar[:], AF.Ln, [B("tvar")], [B("tvar")])
            act(rstd[:], tvar[:], AF.Exp, [B("tvar")], [B("rstd")], scale=-0.5)
            for kc in range(8):
                S.op("dve", lambda e, a=(dst_fn(kc), xsrc_fn(kc), gfn(kc)): e.scalar_tensor_tensor(
                    out=a[0], in0=a[1], scalar=a[2], in1=rstd[:], op0=ALU.mult, op1=ALU.mult),
                    reads=xR + [B("rstd"), B("gains")], writes=dstW)

        def stageA1sq(ci):
            if ci >= 2:
                xload(ci)
            xb = xbuf[ci % 2]
            pieces = []
            for kc in range(8):
                def piece(kc=kc, xb=xb, ci=ci):
                    S.op("act", lambda e, a=(sq[:, kc * 512:(kc + 1) * 512], xb[:, kc * 512:(kc + 1) * 512]): e.activation(out=a[0], in_=a[1], func=AF.Square),
                         reads=[B("xbuf%d%s" % (ci % 2, "" if kc < 4 else "b"))], writes=[B("sq%d" % kc)])
                pieces.append(piece)
            return pieces

        def stageA1rest(ci):
            own = ci >= 4
            t0 = (ci % 4) * 512
            xb = xbuf[ci % 2]
            xB = B("xbuf%d" % (ci % 2))
            hdst = hown if own else hprev
            hB = B("hown%d" % (ci % 4)) if own else B("hprev%d" % (ci % 4))

            def xs(kc, xb=xb):
                return xb[:, :] if kc is None else xb[:, kc * 512:(kc + 1) * 512]

            rms_rest(xs, sq, B("sq"), g1, lambda kc, hd=hdst, t0=t0: hd[:, kc * T + t0:kc * T + t0 + 512], [xB, B("xbuf%db" % (ci % 2))], [hB], 0)

        def stageA2(ci, pieces=()):
            pieces = list(pieces)
            own = ci >= 4
            t0 = (ci % 4) * 512
            hdst = hown if own else hprev
            hB = B("hown%d" % (ci % 4)) if own else B("hprev%d" % (ci % 4))
            for blk in range(4):
                gb = ci * 4 + blk
                tok = t0 + blk * 128
                pv = 1 + blk % 2
                for kc in range(8):
                    mm(ps(pv), hdst[:, kc * T + tok:kc * T + tok + 128], W[0][:, kc * 512:(kc + 1) * 512], kc == 0, kc == 7,
                       [hB, B("W0")], [PB[pv]])
                pv4 = ps(pv).rearrange("p (q e d) -> p q e d", e=2, d=64)
                S.op("act", lambda e, a=(VP4[:, gb, :, 0:64], pv4[:, :, 0, :]): e.copy(a[0], a[1]), reads=[PB[pv]], writes=[B("VP")])
                S.op("act", lambda e, a=(VP4[:, gb, :, 65:129], pv4[:, :, 1, :]): e.copy(a[0], a[1]), reads=[PB[pv]], writes=[B("VP")])
                for _ in range((0, 2, 3, 3)[blk]):
                    if pieces:
                        pieces.pop(0)()
                for kc in range(8):
                    mm(pss[:, 3 * 512 + blk * 8:3 * 512 + blk * 8 + 8], hdst[:, kc * T + tok:kc * T + tok + 128], wf[:, kc * 8:(kc + 1) * 8],
                       kc == 0, kc == 7, [hB, B("wf")], [PB[3]])
            tt("dve", zt[:], pss[:, 3 * 512:3 * 512 + 32], bfb[:], ALU.add, [PB[3]] + bfbR, [B("zt")])
            act(et[:], zt[:], AF.Exp, [B("zt")], [B("et")], scale=-1.0)
            act(spt_all[:, ci * 32:(ci + 1) * 32], et[:], AF.Ln, [B("et")], [B("spt_all%d" % ci)], bias=1.0)

        def stageA2b(ci):
            for b4 in range(4):
                o = pss[:, 4 * 512 + b4 * 8:4 * 512 + b4 * 8 + 8]
                mm(o, trif[:], spt_all[:, ci * 32 + b4 * 8:ci * 32 + (b4 + 1) * 8], True, True, [B("trif"), B("spt_all%d" % ci)], [PB[4]])
            cp("dve", w_all[:, ci * 32:(ci + 1) * 32], pss[:, 4 * 512:4 * 512 + 32], [PB[4]], [B("w_all%d" % ci)])

        def finalize_gates():
            SA = [B("spt_all%d" % i) for i in range(8)]
            WA = [B("w_all%d" % i) for i in range(8)]
            sp3 = spt_all[:, :].rearrange("p (b h) -> p b h", h=8)
            for h in range(8):
                mm(pss[0:32, 4 * 512 + h:4 * 512 + h + 1], sp3[:, :, h], onesf[:, 0:1], True, True, SA + [B("onesf")], [PB[4]])
            cp("dve", totT[0:32, :], pss[0:32, 4 * 512:4 * 512 + 8], [PB[4]], [B("totT")])
            for h in range(8):
                ts("dve", xfin[0:32, h * 32:(h + 1) * 32], bexp[0:32, h * 32:(h + 1) * 32], totT[0:32, h:h + 1], None, ALU.mult, ALU.bypass,
                   [B("bexp"), B("totT")], [B("xfin")])
            mm(pss[:, 4 * 512:4 * 512 + 256], onesf[0:32, :], xfin[0:32, :], True, True, [B("onesf"), B("xfin")], [PB[4]])
            tt("dve", cs_all[:, :].rearrange("p (b h) -> p b h", h=8), w_all[:, :].rearrange("p (b h) -> p b h", h=8),
               pss[:, 4 * 512:4 * 512 + 256].rearrange("p (h b) -> p b h", b=32), ALU.add, WA + [PB[4]], [B("cs_all")])
            cp("dve", hia[:], cs_all[:], [B("cs_all")], [B("hia")])
            tt("dve", r1a[:], cs_all[:], hia[:], ALU.subtract, [B("cs_all"), B("hia")], [B("r1a")])
            cp("dve", loa[:], r1a[:], [B("r1a")], [B("loa")])
            tt("dve", cs_all[:], r1a[:], loa[:], ALU.subtract, [B("r1a"), B("loa")], [B("cs_all")])
            cp("dve", lo2a[:], cs_all[:], [B("cs_all")], [B("lo2a")])
            for j, (tb, tn) in enumerate(((hia, "hia"), (loa, "loa"), (lo2a, "lo2a"))):
                v = tb[:, :].rearrange("p (b h) -> p b h", h=8)
                cp("dve", AK4[:, :, :, 3 + j], v, [B(tn), B("AK")], [B("AK")])
                ts("dve", AQ4[:, :, :, j], v[:, 16:32, :], -8.0, None, ALU.mult, ALU.bypass, [B(tn), B("AQ")], [B("AQ")])

        for pc in stageA1sq(0):
            pc()
        stageA1rest(0)
        for pc in stageA1sq(1):
            pc()
        for ci in range(8):
            if ci + 1 < 8:
                stageA1rest(ci + 1)
            stageA2(ci, stageA1sq(ci + 2) if ci + 2 < 8 else ())
            if ci == 1:
                ld("pool", wview(W[1], 512), wsrc(w_in, C_K, 512), B("W1"))
            if ci >= 1:
                stageA2b(ci - 1)
        stageA2b(7)
        ld("sp", bexp[0:32, :], bexp_d[:, :], B("bexp"))
        finalize_gates()

        S.op("dve", lambda e: e.memset(wsTb[64:128, :].rearrange("p (g i) -> p g i", i=128)[:, :, 0:64], 0.0),
             reads=[B("wsTb")], writes=[B("wsTb")])
        dumps = []

        def dump(name, ap_, rbufs):
            if name in dbg_d:
                dumps.append((name, ap_, rbufs))

        dump("hown", hown[:, 0:512], [B("hown0")])
        dump("AK", AK[:, :], [B("AK")])
        dump("AQ", AQ[:, :], [B("AQ")])
        dump("VP", VP[:, 16 * 516:17 * 516], [B("VP")])

        if stop != "A":
            S.reuse(B("yaT"), [B("xbuf0"), B("xbuf1"), B("xbuf0b"), B("xbuf1b")])
            for i in range(2):
                S.reuse(B("KpT%d" % i), [B("sq%d" % j) for j in range(8)])
            S.reuse(B("QpT0"), [B("sq%d" % j) for j in range(8)])
            for i in range(3):
                S.reuse(B("PT%d" % i), [B(n_) for n_ in ("spt_all%d" % i for i in range(8))] + [B("w_all%d" % i) for i in range(8)] + [B(n_) for n_ in ("cs_all", "r1a", "hia", "loa", "lo2a", "xfin", "totT", "bexp")])
            S.reuse(B("QpT1"), [B(n_) for n_ in ("spt_all%d" % i for i in range(8))] + [B("w_all%d" % i) for i in range(8)] + [B(n_) for n_ in ("cs_all", "r1a", "hia", "loa", "lo2a", "xfin", "totT", "bexp")])
            S.reuse(B("QpT1"), [B("sq%d" % j) for j in range(8)])
            ld("pool", wview(W[0], 512), wsrc(w_in, C_Q, 512), B("W0"))
            S.reuse(B("maskEb"), [B("tvar")])
            S.reuse(B("maskOb"), [B("tvar")])
            ld("pool", maskEb[:], cst_d[:, 512:640], B("maskEb"))
            ld("pool", maskOb[:], cst_d[:, 640:768], B("maskOb"))
            S.reuse(B("negb"), [B("tvar")])
            ld("pool", negb[:], cst_d[:, 768:896], B("negb"))

            def proj_units(h, banks=(7,), split=False):
                units = []
                kb = h % 2
                for ci in range(8):
                    def um(ci=ci, bk=banks[ci % len(banks)]):
                        hsrc, hB = (hprev, B("hprev%d" % ci)) if ci < 4 else (hown, B("hown%d" % (ci - 4)))
                        t0 = (ci % 4) * 512
                        for kc in range(8):
                            mm(pss[0:64, bk * 512:(bk + 1) * 512], W[1][:, kc * 512 + h * 64:kc * 512 + h * 64 + 64],
                               hsrc[:, kc * T + t0:kc * T + t0 + 512], kc == 0, kc == 7, [B("W1"), hB], [PB[bk]])

                    def ur(ci=ci, bk=banks[ci % len(banks)]):
                        for b4 in range(4):
                            mm(pss[64:71, bk * 512 + b4 * 128:bk * 512 + (b4 + 1) * 128], AK4[:, ci * 4 + b4, h, :], identb[:], True, True,
                               [B("AK"), B("identb")], [PB[bk]])
                        cp("dve", KpT[kb][0:71, ci * 512:(ci + 1) * 512], pss[0:71, bk * 512:(bk + 1) * 512], [PB[bk]], [B("KpT%d" % kb)])
                    units.append((um, ur) if split else (lambda um=um, ur=ur: (um(), ur())))
                for ci in range(4):
                    def qm(ci=ci, bk=banks[(ci + 2) % len(banks)]):
                        t0 = ci * 512
                        for kc in range(8):
                            mm(pss[0:64, bk * 512:(bk + 1) * 512], W[0][:, kc * 512 + h * 64:kc * 512 + h * 64 + 64],
                               hown[:, kc * T + t0:kc * T + t0 + 512], kc == 0, kc == 7, [B("W0"), B("hown%d" % ci)], [PB[bk]])

                    def qr(ci=ci, bk=banks[(ci + 2) % len(banks)]):
                        t0 = ci * 512
                        for b4 in range(4):
                            mm(pss[64:71, bk * 512 + b4 * 128:bk * 512 + (b4 + 1) * 128], AQ4[:, ci * 4 + b4, h, :], identb[:], True, True,
                               [B("AQ"), B("identb")], [PB[bk]])
                        cp("dve", QpT[kb][0:71, t0:t0 + 512], pss[0:71, bk * 512:(bk + 1) * 512], [PB[bk]], [B("QpT%d" % kb)])
                    units.append((qm, qr) if split else (lambda qm=qm, qr=qr: (qm(), qr())))
                return units

            up0 = proj_units(0, banks=(0, 1, 2, 3, 4, 5), split=True)
            for g0 in (0, 6):
                for um_, _ in up0[g0:g0 + 6]:
                    um_()
                for _, ur_ in up0[g0:g0 + 6]:
                    ur_()
            yk = [0]
            groups = []
            for h in range(NH):
                for I in range(4):
                    full = list(range(0, 4 * I)) + list(range(16, 16 + 4 * I))
                    glist = [([(full[i], 0), (full[i + 1], 0)], None) for i in range(0, len(full), 2)]
                    for d in (0, 2):
                        glist.append(([(4 * I + d, d * 128), (4 * I + d + 1, (d + 1) * 128)], "EO"))
                    for d in (0, 2):
                        glist.append(([(16 + 4 * I + d, d * 128), (16 + 4 * I + d + 1, (d + 1) * 128)], "TT"))
                    for gi_, (blks, diag) in enumerate(glist):
                        groups.append((h, I, blks, diag, gi_ == 0, gi_ == len(glist) - 1))
            G = len(groups)
            units = {h: proj_units(h) for h in range(1, NH)}
            upos = {h: 0 for h in range(1, NH)}
            per_head = G // NH
            every = max(1, per_head // 13)

            def flush_units(h):
                if h in units:
                    while upos[h] < len(units[h]):
                        units[h][upos[h]]()
                        upos[h] += 1

            def emit_S(k):
                h, I, blks, diag, first, last = groups[k]
                flush_units(h)
                kb = h % 2
                Kt, Qt = KpT[kb], QpT[kb]
                sb = 2 * (k % 3)
                for i, (j, c0) in enumerate(blks):
                    cs0 = blks[0][1] if diag else c0
                    o = pss[:, (sb + i) * 512 + cs0:(sb + i + 1) * 512]
                    r0, r1 = 0, 71
                    mm(o, Kt[r0:r1, j * 128:(j + 1) * 128], Qt[r0:r1, I * 512 + cs0:(I + 1) * 512], True, not diag,
                       [B("KpT%d" % kb), B("QpT%d" % kb)], [PB[sb + i]])
                    if diag:
                        mt, mB = {"T": (trimaskb, "trimaskb"), "E": (maskEb, "maskEb"), "O": (maskOb, "maskOb")}[diag[i]]
                        if cs0 < c0:
                            mm(pss[:, (sb + i) * 512 + cs0:(sb + i) * 512 + c0], identb[:], negb[:], False, False,
                               [B("identb"), B("negb")], [PB[sb + i]])
                        mm(pss[:, (sb + i) * 512 + c0:(sb + i) * 512 + c0 + 128], identb[:], mt[:], False, True,
                           [B("identb"), B(mB)], [PB[sb + i]])

            def emit_E(k):
                h, I, blks, diag, first, last = groups[k]
                sb = 2 * (k % 3)
                n = len(blks)
                c0 = blks[0][1]
                if c0 == 0:
                    o_ap, i_ap = PT[k % 3][:, 0:n * 512], pss[:, sb * 512:(sb + n) * 512]
                else:
                    o_ap = PT[k % 3][:, 0:n * 512].rearrange("p (b c) -> p b c", c=512)[:, :, c0:512]
                    i_ap = pss[:, sb * 512:(sb + n) * 512].rearrange("p (b c) -> p b c", c=512)[:, :, c0:512]
                act(o_ap, i_ap, AF.Exp, [PB[sb + i] for i in range(n)], [B("PT%d" % (k % 3))], scale=0.125)

            pending = []

            def emit_PV(k):
                h, I, blks, diag, first, last = groups[k]
                for i, (j, c0) in enumerate(blks):
                    mrow = 65 if h % 2 == 0 else 128
                    mm(pss[0:mrow, 6 * 512 + c0:7 * 512], vslab(j, h), PT[k % 3][:, i * 512 + c0:(i + 1) * 512],
                       first and i == 0, last and i == len(blks) - 1, [B("VP"), B("PT%d" % (k % 3))], [PB[6]])
                if last:
                    yi = yk[0] % 2
                    y = ysb[yi]
                    yB = B("ysb%d" % yi)
                    yk[0] += 1
                    odd = h % 2
                    lo = 64 * odd
                    drow = 63 if odd else 64
                    if odd:
                        cp("dve", y[0:128, :], pss[0:128, 6 * 512:7 * 512], [PB[6]], [yB])
                    else:
                        cp("dve", y[0:65, :], pss[0:65, 6 * 512:7 * 512], [PB[6]], [yB])
                    r_ = yk[0] - 1
                    S.dma("sp", lambda e, a=(scr_d[r_:r_ + 1, :], y[drow:drow + 1, :]): e.dma_start(out=a[0], in_=a[1]),
                          reads=[yB], writes=[B("scr%d" % yi)], sbuf=B("scr%d" % yi))
                    S.dma("sp", lambda e, a=(rec[yi][lo:lo + 64, :], scr_d[r_:r_ + 1, :].broadcast_to([64, 512])): e.dma_start(out=a[0], in_=a[1]),
                          reads=[B("scr%d" % yi)], writes=[B("bc%d" % yi)], sbuf=B("bc%d" % yi))

                    def normB(h=h, I=I, y=y, yB=yB, yi=yi, lo=lo):
                        S.op("dve", lambda e, a=rec[yi][lo:lo + 64, :]: e.reciprocal(a, a), reads=[B("bc%d" % yi)], writes=[B("bc%d" % yi)])
                        tt("dve", yaT[lo:lo + 64, (h // 2) * T + I * 512:(h // 2) * T + (I + 1) * 512], y[lo:lo + 64, :], rec[yi][lo:lo + 64, :], ALU.mult,
                           [yB, B("bc%d" % yi)], [B("yaT")])
                    tgt = k_cur[0] + 6
                    nb = next((b_ for b_ in boundaries if b_ > k_cur[0]), None)
                    if nb is not None and tgt <= nb < tgt + 5:
                        tgt = nb
                    pending.append([tgt, normB])

            boundaries = [k_ for k_ in range(G) if groups[k_][5]]
            k_cur = [0]
            emit_S(0)
            emit_S(1)
            for k in range(G):
                emit_E(k)
                if k + 2 < G:
                    emit_S(k + 2)
                k_cur[0] = k
                emit_PV(k)
                due = [p for p in pending if p[0] <= k]
                for p in due:
                    pending.remove(p)
                    p[1]()
                h = groups[k][0]
                if h + 1 < NH and (k % per_head) % every == every - 1 and upos[h + 1] < len(units[h + 1]):
                    units[h + 1][upos[h + 1]]()
                    upos[h + 1] += 1
            while pending:
                pending.pop(0)[1]()
            dump("yaT", yaT[:, 0:2048], [B("yaT")])

        if stop not in ("A", "C"):
            for nme in ("vln0", "vln1", "vtmp0", "vtmp1", "stmp0", "stmp1"):
                S.reuse(B(nme), [B("QpT0"), B("QpT1"), B("PT0"), B("PT1"), B("PT2")])
            for i in range(16):
                S.reuse(B("g32_%d" % i), [B("VP")])
            ld("pool", wview(W[0], 512), wsrc(w_in, C_U, 512), B("W0"))
            ld("pool", wview(W[1], 512), wsrc(w_in, C_SV, 512), B("W1"))
            def u_unit(n):
                I, cc = divmod(n, 4)
                pb = n % 2
                for kc in range(8):
                    mm(ps(pb), W[0][:, kc * 512 + cc * 128:kc * 512 + (cc + 1) * 128], hown[:, kc * T + I * 512:kc * T + (I + 1) * 512],
                       kc == 0, kc == 7, [B("W0"), B("hown%d" % I)], [PB[pb]])
                act(uT[:, cc * T + I * 512:cc * T + (I + 1) * 512], ps(pb), AF.Gelu, [PB[pb]], [B("uT%d" % I)])

            for i in range(4):
                S.reuse(B("uT%d" % i), [B("KpT0"), B("KpT1")])
            for blk in range(16):
                pb = 4 + blk % 4
                for kc in range(8):
                    mm(ps(pb), hown[:, kc * T + blk * 128:kc * T + (blk + 1) * 128], W[1][:, kc * 512:(kc + 1) * 512], kc == 0, kc == 7,
                       [B("hown%d" % (blk // 4)), B("W1")], [PB[pb]])
                gsl = g32all[:, blk * 512:(blk + 1) * 512]
                act(gsl, ps(pb), AF.Gelu, [PB[pb]], [B("g32_%d" % blk)])
                S.op("dve", lambda e, a=(st6[:, 0:6], gsl): e.bn_stats(a[0], a[1]), reads=[B("g32_%d" % blk)], writes=[B("st6")])
                S.op("dve", lambda e, a=(mv16[:, blk * 2:blk * 2 + 2], st6[:, 0:6]): e.bn_aggr(a[0], a[1]), reads=[B("st6")], writes=[B("mv16")])
            for n in range(4):
                u_unit(n)
            mvv = mv16[:, :].rearrange("p (b c) -> p b c", c=2)
            ts("dve", tv16[:], mvv[:, :, 1], EPS, None, ALU.add, ALU.bypass, [B("mv16")], [B("tv16")])
            act(tv16[:], tv16[:], AF.Ln, [B("tv16")], [B("tv16")])
            act(rs16[:], tv16[:], AF.Exp, [B("tv16")], [B("rs16")], scale=-0.5)
            def d_norm(blk):
                i2 = blk % 2
                gsl = g32all[:, blk * 512:(blk + 1) * 512]
                S.op("dve", lambda e, a=(vtmp[i2][:], gsl, mv16[:, 2 * blk:2 * blk + 1], lng[:]): e.scalar_tensor_tensor(
                    out=a[0], in0=a[1], scalar=a[2], in1=a[3], op0=ALU.subtract, op1=ALU.mult),
                    reads=[B("g32_%d" % blk), B("mv16"), B("lng")], writes=[B("vtmp%d" % i2)])
                S.op("dve", lambda e, a=(vln[i2][:], vtmp[i2][:], rs16[:, blk:blk + 1], lnb[:]): e.scalar_tensor_tensor(
                    out=a[0], in0=a[1], scalar=a[2], in1=a[3], op0=ALU.mult, op1=ALU.add),
                    reads=[B("vtmp%d" % i2), B("rs16"), B("lnb")], writes=[B("vln%d" % i2)])

            d_norm(0)
            for blk in range(16):
                i2 = blk % 2
                if blk + 4 < 16:
                    u_unit(blk + 4)
                pb = 2 + blk % 2
                for g in range(8):
                    pr, gp = g // 2, g % 2
                    mm(pss[gp * 64:(gp + 1) * 64, pb * 512 + pr * 128:pb * 512 + (pr + 1) * 128], vln[i2][:, g * 64:(g + 1) * 64],
                       wsTb[:, g * 128:(g + 1) * 128], True, True, [B("vln%d" % i2), B("wsTb")], [PB[pb]])
                if blk + 1 < 16:
                    d_norm(blk + 1)
                tt("dve", stmp[i2][:], ps(pb), bsT[:], ALU.add, [PB[pb]] + bsTR, [B("stmp%d" % i2)])
                uv = uT[:, :].rearrange("p (c t) -> p c t", t=T)[:, :, blk * 128:(blk + 1) * 128]
                tt("dve", uv, stmp[i2][:, :].rearrange("p (c t) -> p c t", t=128), uv, ALU.mult, [B("stmp%d" % i2), B("uT%d" % (blk // 4))], [B("uT%d" % (blk // 4))])
            dump("sT", uT[:, 0:2048], [B("uT0")])

        if stop not in ("A", "C", "D"):
            S.reuse(B("mergedT"), [B("hprev%d" % i) for i in range(4)])
            for i in range(2):
                for j in range(2):
                    S.reuse(B("sig%d%d" % (i, j)), [B("g32_%d" % i) for i in range(16)])
                    S.reuse(B("m1%d%d" % (i, j)), [B("g32_%d" % i) for i in range(16)])
            for s_ in range(2):
                for n_ in ("Wb%d", "Wbb%d", "Wbga%d", "Wbgb%d"):
                    S.reuse(B(n_ % s_), [B("W%d" % s_)])

            def bslot(dc):
                s = dc % 2
                if dc < 6:
                    return W[s], ["Wb%d" % s, "Wbb%d" % s, "Wbga%d" % s, "Wbgb%d" % s]
                return Walt[s], ["Xb%d" % s, "Xbb%d" % s, "Xbga%d" % s, "Xbgb%d" % s]

            for s_ in range(2):
                for n_ in ("Xb%d", "Xbb%d", "Xbga%d", "Xbgb%d"):
                    S.reuse(B(n_ % s_), [B("g32_%d" % i) for i in range(16)])

            def bundle(dc):
                wt, nm = bslot(dc)
                c0 = dc * 128
                ld("pool", wt[:, 0:512].rearrange("p (k n) -> p k n", n=128), wsrc(w_a, c0, 128, nk=4), B(nm[0]))
                ld("pool", wt[:, 1024:1536].rearrange("p (k n) -> p k n", n=128), wsrc(w_b, c0, 128, nk=4), B(nm[1]))
                ld("pool", wt[:, 1536:2560].rearrange("p (k n) -> p k n", n=128), wsrc(w_in, C_GA + c0, 128), B(nm[2]))
                ld("pool", wt[:, 2560:3584].rearrange("p (k n) -> p k n", n=128), wsrc(w_in, C_GB + c0, 128), B(nm[3]))

            bundle(0)
            k = 0
            for dc in range(8):
                wt, nm = bslot(dc)
                if dc + 1 < 8:
                    bundle(dc + 1)
                if dc == 5:
                    S.reuse(B("Wo0"), [B(n % 0) for n in ("Wb%d", "Wbb%d", "Wbga%d", "Wbgb%d")])
                    ld("pool", Wo[:, :].rearrange("p (kc n) -> p kc n", n=1024)[:, 0:4, :], wsrc(w_o, 0, 1024, r0=0, nk=4), B("Wo0"))
                if dc == 6:
                    S.reuse(B("Wo1"), [B(n % 1) for n in ("Wb%d", "Wbb%d", "Wbga%d", "Wbgb%d")])
                    ld("pool", Wo[:, :].rearrange("p (kc n) -> p kc n", n=1024)[:, 4:8, :], wsrc(w_o, 0, 1024, r0=512, nk=4), B("Wo1"))
                for I in range(4):
                    p0 = 4 * (k % 2)
                    k2 = k % 2
                    k += 1
                    tsl = slice(I * 512, (I + 1) * 512)
                    for kc in range(4):
                        mm(ps(p0), wt[:, kc * 128:(kc + 1) * 128], yaT[:, kc * T + I * 512:kc * T + (I + 1) * 512], kc == 0, kc == 3,
                           [B(nm[0]), B("yaT")], [PB[p0]])
                    for kc in range(4):
                        mm(ps(p0 + 1), wt[:, 1024 + kc * 128:1024 + (kc + 1) * 128], uT[:, kc * T + I * 512:kc * T + (I + 1) * 512], kc == 0, kc == 3,
                           [B(nm[1]), B("uT%d" % I)], [PB[p0 + 1]])
                    for kc in range(8):
                        mm(ps(p0 + 2), wt[:, 1536 + kc * 128:1536 + (kc + 1) * 128], hown[:, kc * T + I * 512:kc * T + (I + 1) * 512], kc == 0, kc == 7,
                           [B(nm[2]), B("hown%d" % I)], [PB[p0 + 2]])
                    for kc in range(8):
                        mm(ps(p0 + 3), wt[:, 2560 + kc * 128:2560 + (kc + 1) * 128], hown[:, kc * T + I * 512:kc * T + (I + 1) * 512], kc == 0, kc == 7,
                           [B(nm[3]), B("hown%d" % I)], [PB[p0 + 3]])
                    act(sig[k2][0][:], ps(p0 + 2), AF.Sigmoid, [PB[p0 + 2]], [B("sig%d0" % k2)])
                    act(sig[k2][1][:], ps(p0 + 3), AF.Sigmoid, [PB[p0 + 3]], [B("sig%d1" % k2)])
                    tt("dve", m1[k2][0][:], sig[k2][0][:], ps(p0), ALU.mult, [B("sig%d0" % k2), PB[p0]], [B("m1%d0" % k2)])
                    tt("dve", m1[k2][1][:], sig[k2][1][:], ps(p0 + 1), ALU.mult, [B("sig%d1" % k2), PB[p0 + 1]], [B("m1%d1" % k2)])
                    tt("pool", mergedT[:, dc * T + I * 512:dc * T + (I + 1) * 512], m1[k2][0][:], m1[k2][1][:], ALU.add,
                       [B("m1%d0" % k2), B("m1%d1" % k2)], [B("mergedT")])
            dump("mergedT", mergedT[:, 0:2048], [B("mergedT")])

            for dc_ in range(8):
                for i_ in range(4):
                    S.reuse(B("x1T%d_%d" % (dc_, i_)), [B("hown%d" % i) for i in range(4)] + [B("yaT")])
            for i in range(3):
                S.reuse(B("xres%d" % i), [B(n_) for n_ in ("vln0", "vln1", "vtmp0", "vtmp1", "stmp0", "stmp1", "PT0", "PT1", "PT2", "QpT0", "QpT1")])
            for kc_ in range(8):
                S.reuse(B("sq2_%d" % kc_), [B(n_) for n_ in ("vln0", "vln1", "vtmp0", "vtmp1", "stmp0", "stmp1", "PT0", "PT1", "PT2", "QpT0", "QpT1")])
            Wo3 = Wo[:, :].rearrange("p (kc n) -> p kc n", n=1024)
            xres7 = xres + [ysb[0], ysb[1], rec[0], rec[1]]
            for i in range(3, 7):
                S.reuse(B("xres%d" % i), [B("ysb0"), B("ysb1"), B("rec0"), B("rec1"), B("bc0"), B("bc1")])
            S.reuse(B("tvar"), [B("maskEb"), B("maskOb"), B("negb")])
            for i_ in range(4):
                S.reuse(B("h2T%d" % i_), [B(n_ % s_) for s_ in range(2) for n_ in ("Xb%d", "Xbb%d", "Xbga%d", "Xbgb%d")] + [B("m1%d%d" % (a, b)) for a in range(2) for b in range(2)] + [B("sig%d%d" % (a, b)) for a in range(2) for b in range(2)] + [B("g32_%d" % i) for i in range(16)])
            for i in range(2):
                S.reuse(B("aT%d" % i), [B("mergedT")])
                S.reuse(B("rl%d" % i), [B("mergedT")])
                S.reuse(B("W%d" % (2 + i)), [B("uT%d" % j) for j in range(4)] + [B("vln0"), B("vln1")])
            WU = [W[0], W23[0]]
            WD = [W[1], W23[1]]
            WUB = [B("W0"), B("W2")]
            WDB = [B("W1"), B("W3")]

            def ffn_load(e8):
                s = (e8 + 1) % 2
                ld("pool", wview(WU[s], 512), wsrc(w_up, e8 * 512, 512), WUB[s])
                ld("pool", WD[s][:, :].rearrange("p (k n) -> p k n", n=1024),
                   w_down[e8 * 512:(e8 + 1) * 512, :].rearrange("(k p) n -> p k n", p=128), WDB[s])

            def xsF(I):
                def xs(kc):
                    if kc is None:
                        return x1T[:, :].rearrange("p (kc t) -> p kc t", t=T)[:, :, I * 512:(I + 1) * 512]
                    return x1T[:, kc * T + I * 512:kc * T + (I + 1) * 512]
                return xs

            def normA(I):
                xs = xsF(I)
                for kc in range(8):
                    S.op("act", lambda e, a=(sq2[:, kc * 512:(kc + 1) * 512], xs(kc)): e.activation(out=a[0], in_=a[1], func=AF.Square),
                         reads=[B("x1T%d_%d" % (kc, I))], writes=[B("sq2_%d" % kc)])

            def normB_pieces(I, gfn, dst_fn, dstB_fn, post=None):
                xs = xsF(I)

                def pre():
                    for kc in range(8):
                        mm(ps(7), onesb[:], sq2[:, kc * 512:(kc + 1) * 512], kc == 0, kc == 7, [B("sq2_%d" % kc), B("onesb")], [PB[7]])
                    ts("dve", tvar[:], ps(7), 1.0 / D, EPS, ALU.mult, ALU.add, [PB[7]], [B("tvar")])
                    act(tvar[:], tvar[:], AF.Ln, [B("tvar")], [B("tvar")])
                    act(rstd[:], tvar[:], AF.Exp, [B("tvar")], [B("rstd")], scale=-0.5)

                def mk(kc):
                    def piece():
                        S.op("dve", lambda e, a=(dst_fn(kc), xs(kc), gfn(kc)): e.scalar_tensor_tensor(
                            out=a[0], in0=a[1], scalar=a[2], in1=rstd[:], op0=ALU.mult, op1=ALU.mult),
                            reads=[B("x1T%d_%d" % (kc, I)), B("rstd"), B("gains")], writes=[dstB_fn(kc)])
                        if post is not None:
                            post(kc)
                    return piece
                return [pre] + [mk(kc) for kc in range(8)]

            def normB(I, gfn, dst_fn, dstB_fn, post=None):
                for pc in normB_pieces(I, gfn, dst_fn, dstB_fn, post):
                    pc()

            def rms2B(I):
                normB(I, g2, lambda kc: h2T[:, kc * T + I * 512:kc * T + (I + 1) * 512], lambda kc: B("h2T%d" % I))

            ffn_load(0)
            k = 0
            for I in range(4):
                pcs = normB_pieces(I - 1, g2, lambda kc, I_=I - 1: h2T[:, kc * T + I_ * 512:kc * T + (I_ + 1) * 512], lambda kc, I_=I - 1: B("h2T%d" % I_)) if I >= 1 else []
                for dc in range(8):
                    pb = k % 7
                    xr = xres7[k % 7]
                    xrB = B("xres%d" % (k % 7))
                    k += 1
                    ld("sp", xr[:], xown[dc * 128:(dc + 1) * 128, I * 512:(I + 1) * 512], xrB)
                    for kc in range(8):
                        mm(ps(pb), Wo[:, kc * 1024 + dc * 128:kc * 1024 + (dc + 1) * 128], mergedT[:, kc * T + I * 512:kc * T + (I + 1) * 512],
                           kc == 0, kc == 7, [B("Wo%d" % (kc // 4)), B("mergedT")], [PB[pb]])
                    tt("dve", x1T[:, dc * T + I * 512:dc * T + (I + 1) * 512], ps(pb), xr[:], ALU.add, [PB[pb], xrB], [B("x1T%d_%d" % (dc, I))])
                    if pcs:
                        pcs.pop(0)()
                    if dc == 7:
                        while pcs:
                            pcs.pop(0)()
                        normA(I)
            dump("x1T", x1T[:, 0:2048], [B("x1T0_0")])

        if stop not in ("A", "C", "D", "E"):
            NF = 32 if stop != "F1" else 0

            def ffn_up(n):
                e8, I = divmod(n, 4)
                s = (e8 + 1) % 2
                a2 = n % 2
                for fc in range(4):
                    for kc in range(8):
                        mm(ps(fc), WU[s][:, kc * 512 + fc * 128:kc * 512 + (fc + 1) * 128], h2T[:, kc * T + I * 512:kc * T + (I + 1) * 512],
                           kc == 0, kc == 7, [WUB[s], B("h2T%d" % I)], [PB[fc]])
                    act(rl[a2][:, fc * 512:(fc + 1) * 512], ps(fc), AF.Relu, [PB[fc]], [B("rl%d_%d" % (a2, fc))])
                    tt("pool", aT[a2][:, fc * 512:(fc + 1) * 512], rl[a2][:, fc * 512:(fc + 1) * 512], rl[a2][:, fc * 512:(fc + 1) * 512], ALU.mult,
                       [B("rl%d_%d" % (a2, fc))], [B("aT%d_%d" % (a2, fc))])

            def ffn_down(n, mid=None):
                mid = list(mid) if mid else []
                e8, I = divmod(n, 4)
                s = (e8 + 1) % 2
                a2 = n % 2
                for dc in range(8):
                    pb = 4 + dc % 4
                    for fc in range(4):
                        mm(ps(pb), WD[s][:, fc * 1024 + dc * 128:fc * 1024 + (dc + 1) * 128], aT[a2][:, fc * 512:(fc + 1) * 512],
                           fc == 0, fc == 3, [WDB[s], B("aT%d_%d" % (a2, fc))], [PB[pb]])
                    xsl = x1T[:, dc * T + I * 512:dc * T + (I + 1) * 512]
                    tt("dve", xsl, ps(pb), xsl, ALU.add, [PB[pb], B("x1T%d_%d" % (dc, I))], [B("x1T%d_%d" % (dc, I))])
                    if mid:
                        mid.pop(0)()

            for i in range(2):
                for fc in range(4):
                    S.reuse(B("aT%d_%d" % (i, fc)), [B("mergedT")])
                    S.reuse(B("rl%d_%d" % (i, fc)), [B("mergedT")])
            ostg = [ysb[0], ysb[1], rec[0], rec[1]]
            for i in range(4):
                S.reuse(B("ostg%d" % i), [B("ysb0"), B("ysb1"), B("rec0"), B("rec1"), B("bc0"), B("bc1")] + [B("xres%d" % j) for j in range(3, 7)])
            ocnt = [0]

            def finalB_pieces(I):
                def post(kc):
                    oi = ocnt[0] % 4
                    S.dma("sp", lambda e, a=(out_d[kc * 128:(kc + 1) * 128, I * 512:(I + 1) * 512], ostg[oi][:]): e.dma_start(out=a[0], in_=a[1]),
                          reads=[B("ostg%d" % oi)], sbuf=B("ostg%d" % oi))
                    ocnt[0] += 1
                return normB_pieces(I, gf, lambda kc: ostg[(ocnt[0]) % 4][:], lambda kc: B("ostg%d" % (ocnt[0] % 4)), post=post)

            def finalB(I):
                for pc in finalB_pieces(I):
                    pc()

            S.reuse(B("W0"), [B("Wo0"), B("Wo1")])
            S.reuse(B("W1"), [B("Wo0"), B("Wo1")])
            if NF:
                ffn_load(1)
                ffn_up(0)
            rms2B(3)
            for n in range(NF):
                if n + 1 < NF:
                    ffn_up(n + 1)
                fin = stop not in ("F1", "F2") and n >= 28
                rest = finalB_pieces(n - 29) if (fin and n >= 29) else []
                ffn_down(n, mid=rest[:8])
                for pc in rest[8:]:
                    pc()
                if n % 4 == 3 and n // 4 + 2 < 8:
                    ffn_load(n // 4 + 2)
                if fin:
                    normA(n - 28)
            dump("x2T", x1T[:, 0:2048], [B("x1T0_0")])
            dump("h2T", h2T[:, 0:2048], [B("h2T0")])
            if stop not in ("F1", "F2"):
                finalB(3)
                for i in range(4):
                    S.op("sp", None, writes=[B("ostg%d" % i)])

        if dumps:
            dstage = at("dstage", [128, 4096], F32, OFF_T)
            for name, ap_, rb in dumps:
                ncol = ap_.shape[1]
                np_ = ap_.shape[0]
                dsB = B("dstage")
                S.reuse(dsB, list(Bn.values()))
                cp("dve", dstage[0:np_, 0:ncol], ap_, rb, [dsB])
                S.dma("sp", lambda e, a=(dbg_d[name], dstage[0:np_, 0:ncol]): e.dma_start(out=a[0][:, :], in_=a[1]), reads=[dsB], sbuf=dsB)
            S.op("sp", None, writes=[B("dstage")])
        if stop is not None:
            pass
        S.emit(st)
    return nc


def _consts():
    ident = np.eye(128, dtype=np.float32)
    tri = np.triu(np.ones((128, 128), np.float32))
    ones = np.ones((128, 128), np.float32)
    trimask = np.where(np.arange(128)[:, None] <= np.arange(128)[None, :], 0.0, NEG).astype(np.float32)
    return [ident, tri, ones, trimask]


def _blocks(r):
    own = sorted([4 * j + (0 if r == 0 else 1) for j in range(8)] + [4 * j + (3 if r == 0 else 2) for j in range(8)])
    oth = sorted(set(range(32)) - set(own))
    return own, oth


def make_in_maps(x, norm1_g, w_in, b_f, ln_v_g, ln_v_b, w_sgu, b_sgu, w_a, w_b, w_o, norm2_g, w_up, w_down, normf_g):
    f = lambda a: np.ascontiguousarray(np.asarray(a, dtype=np.float32))
    x = f(x)
    g3 = np.concatenate([f(norm1_g)[0].reshape(8, 128).T, f(norm2_g)[0].reshape(8, 128).T, f(normf_g).reshape(8, 128).T], axis=1)
    common = {
        "w_in": f(w_in)[0], "w_a": f(w_a)[0], "w_b": f(w_b)[0], "w_o": f(w_o)[0], "w_up": f(w_up)[0], "w_down": f(w_down)[0],
        "gains": f(g3), "b_f": f(b_f).reshape(1, 8), "lnv": f(np.stack([f(ln_v_g)[0], f(ln_v_b)[0]])),
        "wsT": f(np.transpose(f(w_sgu)[0], (2, 0, 1)).reshape(128, 8 * 128)), "b_sgu": f(b_sgu)[0], "cst": _consts(),
    }
    common.pop("cst")
    maps = []
    for c in range(8):
        b, r = c // 2, c % 2
        own, oth = _blocks(r)
        gat = lambda blks: f(np.concatenate([x[b, g * 128:(g + 1) * 128, :] for g in blks], axis=0).T)
        glob = np.array(oth + own)
        before = (glob[:, None] < glob[None, :]).astype(np.float32)
        bexp = np.ascontiguousarray(np.tile(before, (1, 8)))
        full = np.full((128, 128), NEG, np.float32)
        zero = np.zeros((128, 128), np.float32)
        cst = np.ascontiguousarray(np.concatenate(_consts() + ([full, zero] if r == 0 else [zero, full]) + [full], axis=1))
        m = dict(common)
        m.update({"xprev": gat(oth), "xown": gat(own), "kmask": np.zeros((128, 32), np.float32), "cst": cst, "bexp": bexp})
        maps.append(m)
    return maps


_NC_CACHE = {}


def kernel(**inputs):
    if "nc" not in _NC_CACHE:
        _NC_CACHE["nc"] = build()
    nc = _NC_CACHE["nc"]
    maps = make_in_maps(**inputs)
    res = run_bass_kernel_spmd(nc, maps, core_ids=list(range(8)))
    out = np.empty((4, 4096, D), np.float32)
    for c in range(8):
        b, r = c // 2, c % 2
        own, _ = _blocks(r)
        o = res.results[c]["out"].T
        for l, g in enumerate(own):
            out[b, g * 128:(g + 1) * 128, :] = o[l * 128:(l + 1) * 128, :]
    return out
```

```python
import contextlib
import numpy as np
import concourse.bass as bass
import concourse.mybir as mybir
from concourse.bass_utils import run_bass_kernel_spmd

F32 = mybir.dt.float32
BF16 = mybir.dt.bfloat16
AF = mybir.ActivationFunctionType
ALU = mybir.AluOpType

D = 1024
T = 2048
NH = 8
EPS = 1e-6
NEG = -240000.0
KMASK = -30000.0


class Buf:
    __slots__ = ("name", "w", "r", "rd", "cnt")

    def __init__(self, name):
        self.name = name
        self.w = None
        self.r = {}
        self.rd = []
        self.cnt = 0


class Op:
    __slots__ = ("eng", "fn", "deps", "sem", "val", "signal", "dma")

    def __init__(self, eng, fn, dma=False):
        self.eng = eng
        self.fn = fn
        self.deps = []
        self.sem = None
        self.val = None
        self.signal = False
        self.dma = dma


class Sched:
    ENGS = ("pe", "act", "dve", "pool", "sp")

    def __init__(self, nc):
        self.nc = nc
        self.ops = {e: [] for e in self.ENGS}

    def _add(self, o, reads, writes):
        deps = {}
        for b in reads:
            if b.w is not None:
                deps[id(b.w)] = b.w
        for b in writes:
            if b.w is not None:
                deps[id(b.w)] = b.w
            for r in b.r.values():
                deps[id(r)] = r
            for r in b.rd:
                deps[id(r)] = r
        for d in deps.values():
            if d is o:
                continue
            if d.eng == "pe" and o.eng == "pe" and not d.dma and not o.dma:
                continue
            o.deps.append(d)
            d.signal = True
        for b in reads:
            if o.dma:
                b.rd.append(o)
            else:
                b.r[o.eng] = o
        for b in writes:
            b.w = o
            b.r = {}
            b.rd = []
        self.ops[o.eng].append(o)
        return o

    def op(self, eng, fn, reads=(), writes=()):
        return self._add(Op(eng, fn), reads, writes)

    def dma(self, eng, fn, reads=(), writes=(), sbuf=None):
        o = Op(eng, fn, dma=True)
        o.sem = sbuf
        sbuf.cnt += 1
        o.val = 16 * sbuf.cnt
        return self._add(o, reads, writes)

    def reuse(self, new, olds):
        for b in olds:
            if b.w is not None:
                new.rd.append(b.w)
            new.rd.extend(b.r.values())
            new.rd.extend(b.rd)

    def emit(self, stack):
        nc = self.nc
        eng_sem = {e: stack.enter_context(nc.semaphore("s_" + e)) for e in self.ENGS}
        bufsems = {}
        for e in self.ENGS:
            c = 0
            for o in self.ops[e]:
                if o.dma:
                    b = o.sem
                    if id(b) not in bufsems:
                        bufsems[id(b)] = stack.enter_context(nc.semaphore("d_" + b.name))
                    o.sem = bufsems[id(b)]
                elif o.signal:
                    c += 1
                    o.sem = eng_sem[e]
                    o.val = c
        handles = {"pe": "tensor", "act": "scalar", "dve": "vector", "pool": "gpsimd", "sp": "sync"}
        block = stack.enter_context(nc.Block())
        for e in self.ENGS:
            ops = self.ops[e]
            if not ops:
                continue

            def body(h, ops=ops):
                waited = {}
                for o in ops:
                    for d in o.deps:
                        k = id(d.sem)
                        if waited.get(k, -1) >= d.val:
                            continue
                        h.wait_ge(d.sem, d.val)
                        waited[k] = d.val
                    if o.fn is None:
                        continue
                    ins = o.fn(h)
                    if o.dma:
                        ins.then_inc(o.sem, 16)
                    elif o.signal:
                        ins.then_inc(o.sem, 1)

            getattr(block, handles[e])(body)


KB = 1024
OFF_P = 0
OFF_T = 16 * KB
OFF_W = 28 * KB
OFF_RA = 44 * KB
OFF_RB = 76 * KB
OFF_RC = 108 * KB
OFF_RD = 140 * KB
OFF_RE = 174 * KB
ARENA = 206 * KB

C_Q, C_K, C_V, C_F, C_U, C_SV, C_GA, C_GB = 0, 512, 1024, 1536, 1544, 2056, 2568, 3592


def build(stop=None, dbg=()):
    nc = bass.Bass("TRN2", target_bir_lowering=False)
    dt_in = lambda n, s: nc.dram_tensor(n, s, F32, kind="ExternalInput").ap()
    xprev = dt_in("xprev", [D, T])
    xown = dt_in("xown", [D, T])
    kmask_d = dt_in("kmask", [128, 32])
    w_in = dt_in("w_in", [D, 4616])
    w_a = dt_in("w_a", [512, D])
    w_b = dt_in("w_b", [512, D])
    w_o = dt_in("w_o", [D, D])
    w_up = dt_in("w_up", [D, 4096])
    w_down = dt_in("w_down", [4096, D])
    gains_d = dt_in("gains", [128, 24])
    bf_d = dt_in("b_f", [1, 8])
    lnv_d = dt_in("lnv", [2, 512])
    lnbT_d = dt_in("lnbT", [128, 4])
    wsT_d = dt_in("wsT", [128, 8 * 128])
    bsgu_d = dt_in("b_sgu", [8, 128])
    cst_d = dt_in("cst", [128, 7 * 128])
    bexp_d = dt_in("bexp", [32, 256])
    out_d = nc.dram_tensor("out", [D, T], F32, kind="ExternalOutput").ap()
    scr_d = nc.dram_tensor("scr", [32, 512], F32, kind="Internal").ap()
    dbg_d = {n: nc.dram_tensor("dbg_" + n, list(shp), F32, kind="ExternalOutput").ap() for n, shp in dbg}

    st = contextlib.ExitStack()
    with st:
        S = Sched(nc)
        arena = nc.alloc_sbuf_tensor("arena", [128, ARENA // 4], F32)
        base = nc.lookup_mloc(arena).addr
        cur = {}

        def at(name, shape, dt, off):
            return nc.alloc_sbuf_tensor_at(name, shape, dt, offset=base + off)

        poff = [OFF_P]

        def palloc(name, shape, dt):
            nbytes = shape[1] * (4 if dt == F32 else 2)
            nbytes = (nbytes + 31) // 32 * 32
            t = at(name, shape, dt, poff[0])
            poff[0] += nbytes
            assert poff[0] <= OFF_T
            return t

        identb = palloc("identb", [128, 128], BF16)
        trimaskb = palloc("trimaskb", [128, 128], BF16)
        onesb = palloc("onesb", [128, 128], BF16)
        trif = palloc("trif", [128, 128], F32)
        onesf = palloc("onesf", [128, 128], F32)
        gains = palloc("gains", [128, 24], F32)
        bfb = palloc("bfb", [128, 32], F32)
        lng = palloc("lng", [128, 512], F32)
        lnb = palloc("lnb", [128, 512], F32)
        wsTb = palloc("wsTb", [128, 8 * 128], BF16)
        bsT = palloc("bsT", [128, 4 * 128], F32)
        wf = palloc("wf", [128, 64], BF16)
        AK = palloc("AK", [128, 32 * 56], BF16)
        AQ = palloc("AQ", [128, 16 * 56], BF16)
        acc = palloc("acc", [128, 8], F32)
        pref2 = [palloc("pref", [128, 32], F32), at("pref1", [128, 32], F32, OFF_T + 10 * KB + 1152)]
        kmask = palloc("kmask", [128, 32], F32)
        lnbT = palloc("lnbT", [128, 4], F32)
        mv16 = palloc("mv16", [128, 32], F32)
        rs16 = palloc("rs16", [128, 16], F32)
        nmr16 = palloc("nmr16", [128, 16], F32)
        st6 = palloc("st6", [128, 8], F32)

        rstd = at("rstd", [128, 512], F32, OFF_T)
        tvar = at("tvar", [128, 512], F32, OFF_T + 2 * KB)
        ysb = [at("ysb%d" % i, [128, 512], F32, OFF_T + (4 + 2 * i) * KB) for i in range(2)]
        rec = [at("rec0", [128, 512], F32, 204 * KB), at("rec1", [128, 512], F32, OFF_T + 8 * KB)]
        sm = OFF_T + 10 * KB
        zt = at("zt", [128, 32], F32, sm)
        et = at("et", [128, 32], F32, sm + 128)
        spt2 = [at("spt", [128, 32], F32, sm + 256), at("spt1", [128, 32], F32, sm + 1024)]
        hib = at("hib", [128, 32], BF16, sm + 384)
        lob = at("lob", [128, 32], BF16, sm + 448)
        lo2b = at("lo2b", [128, 32], BF16, sm + 512)
        r1 = at("r1", [128, 32], F32, sm + 576)
        r2 = at("r2", [128, 32], F32, sm + 704)
        tv16 = at("tv16", [128, 16], F32, sm + 832)

        W = [at("W%d" % i, [128, 4096], BF16, OFF_W + 8 * KB * i) for i in range(2)]
        Wo = at("Wo", [128, 8192], BF16, OFF_W)
        Walt = [at("Walt%d" % i, [128, 4096], BF16, OFF_RD + (18 + 8 * i) * KB) for i in range(2)]
        W23 = [at("W%d" % (2 + i), [128, 4096], BF16, OFF_RE + 8 * KB * i) for i in range(2)]

        hown = at("hown", [128, 8 * T], BF16, OFF_RA)
        hprev = at("hprev", [128, 8 * T], BF16, OFF_RC)
        xbuf = [at("xbuf%d" % i, [128, 8 * 512], F32, OFF_RB + 16 * KB * i) for i in range(2)]
        sq = at("sq", [128, 8 * 512], BF16, OFF_RE)
        VP = at("VP", [128, 32 * 516], BF16, OFF_RD)
        KpT = [at("KpT%d" % i, [128, 4096], BF16, OFF_RE + 8 * KB * i) for i in range(2)]
        QpT = [at("QpT%d" % i, [128, T], BF16, OFF_RE + 16 * KB + 4 * KB * i) for i in range(2)]
        PT = [at("PT%d" % i, [128, 1024], BF16, OFF_RE + 24 * KB + 2 * KB * i) for i in range(3)]
        yaT = at("yaT", [128, 4 * T], BF16, OFF_RB)
        uT = at("uT", [128, 4 * T], BF16, OFF_RE)
        vln = [at("vln%d" % i, [128, 512], BF16, OFF_RE + 16 * KB + KB * i) for i in range(2)]
        vtmp = [at("vtmp%d" % i, [128, 512], F32, OFF_RE + 18 * KB + 2 * KB * i) for i in range(2)]
        stmp = [at("stmp%d" % i, [128, 512], F32, OFF_RE + 22 * KB + 2 * KB * i) for i in range(2)]
        g32all = at("g32all", [128, 16 * 512], F32, OFF_RD)
        mergedT = at("mergedT", [128, 8 * T], BF16, OFF_RC)
        sig = [[at("sig%d%d" % (i, j), [128, 512], F32, OFF_RD + (4 * i + 2 * j) * KB) for j in range(2)] for i in range(2)]
        m1 = [[at("m1%d%d" % (i, j), [128, 512], F32, OFF_RD + (8 + 4 * i + 2 * j) * KB) for j in range(2)] for i in range(2)]
        xres = [at("xres%d" % i, [128, 512], F32, OFF_RE + (16 + 2 * i) * KB) for i in range(3)]
        x1T = at("x1T", [128, 8 * T], F32, OFF_RA)
        h2T = at("h2T", [128, 8 * T], BF16, OFF_RD)
        aT = [at("aT%d" % i, [128, 4 * 512], BF16, OFF_RC + 4 * KB * i) for i in range(2)]
        rl = [at("rl%d" % i, [128, 4 * 512], F32, OFF_RC + 8 * KB + 8 * KB * i) for i in range(2)]
        sq2 = at("sq2", [128, 8 * 512], BF16, OFF_RE + 22 * KB)
        ostage = at("ostage", [128, 8 * 512], F32, OFF_RC)

        FZ = OFF_RE + 24 * KB
        spt_all = at("spt_all", [128, 256], F32, FZ)
        w_all = at("w_all", [128, 256], F32, FZ + 1 * KB)
        cs_all = at("cs_all", [128, 256], F32, FZ + 2 * KB)
        r1a = at("r1a", [128, 256], F32, FZ + 3 * KB)
        hia = at("hia", [128, 256], BF16, FZ + 4 * KB)
        loa = at("loa", [128, 256], BF16, FZ + 4 * KB + 512)
        lo2a = at("lo2a", [128, 256], BF16, FZ + 5 * KB)
        xfin = at("xfin", [128, 256], F32, OFF_RE + 21 * KB)
        totT = at("totT", [128, 8], F32, OFF_RE + 23 * KB)
        bexp = at("bexp", [128, 256], F32, OFF_RE + 22 * KB)
        maskEb = at("maskEb", [128, 128], BF16, OFF_T + 2 * KB)
        maskOb = at("maskOb", [128, 128], BF16, OFF_T + 2 * KB + 256)
        negb = at("negb", [128, 128], BF16, OFF_T + 2 * KB + 512)
        pss = nc.alloc_psum_tensor("pss", [128, 8 * 512], F32)

        def ps(b, n=1):
            return pss[:, b * 512:(b + n) * 512]

        Bn = {}

        def B(name):
            if name not in Bn:
                Bn[name] = Buf(name)
            return Bn[name]

        PB = [B("psb%d" % i) for i in range(8)]

        def mm(out, lhsT, rhs, start, stop, r, w):
            S.op("pe", lambda e, a=(out, lhsT, rhs, start, stop): e.matmul(a[0], a[1], a[2], start=a[3], stop=a[4]),
                 reads=r, writes=w)

        def act(out, in_, func, r, w, scale=1.0, bias=None):
            if bias is None:
                S.op("act", lambda e, a=(out, in_, func, scale): e.activation(out=a[0], in_=a[1], func=a[2], scale=a[3]),
                     reads=r, writes=w)
            else:
                S.op("act", lambda e, a=(out, in_, func, scale, bias): e.activation(out=a[0], in_=a[1], func=a[2], scale=a[3], bias=a[4]),
                     reads=r, writes=w)

        def tt(eng, out, in0, in1, op, r, w):
            S.op(eng, lambda e, a=(out, in0, in1, op): e.tensor_tensor(a[0], a[1], a[2], a[3]), reads=r, writes=w)

        def ts(eng, out, in0, s1, s2, op0, op1, r, w):
            S.op(eng, lambda e, a=(out, in0, s1, s2, op0, op1): e.tensor_scalar(a[0], a[1], a[2], a[3], a[4], a[5]), reads=r, writes=w)

        def cp(eng, out, in_, r, w):
            S.op(eng, lambda e, a=(out, in_): e.tensor_copy(a[0], a[1]), reads=r, writes=w)

        def ld(eng, out, in_, wbuf, r=()):
            S.dma(eng, lambda e, a=(out, in_): e.dma_start(out=a[0], in_=a[1]), reads=r, writes=[wbuf], sbuf=wbuf)

        def wview(t, n):
            return t[:, 0:8 * n].rearrange("p (kc n) -> p kc n", n=n)

        def wsrc(w, c0, n, r0=0, nk=8):
            return w[r0:r0 + nk * 128, c0:c0 + n].rearrange("(kc p) n -> p kc n", p=128)

        def xload(ci, r=()):
            src = xown if ci >= 4 else xprev
            t0 = (ci % 4) * 512
            i_ = ci % 2
            dst3 = xbuf[i_][:, :].rearrange("p (kc t) -> p kc t", t=512)
            src3 = src[:, t0:t0 + 512].rearrange("(kc p) t -> p kc t", p=128)
            ba, bb = B("xbuf%d" % i_), B("xbuf%db" % i_)
            if ci < 2:
                ld("sp", dst3[:, 0:4, :], src3[:, 0:4, :], ba, r=r)
                ld("act", dst3[:, 4:8, :], src3[:, 4:8, :], bb, r=r)
            else:
                S.dma("sp", lambda e, a=(dst3, src3): e.dma_start(out=a[0], in_=a[1]), writes=[ba, bb], sbuf=ba)

        ld("sp", gains[:], gains_d[:, :], B("gains"))
        ld("sp", kmask[:], kmask_d[:, :], B("kmask"))
        xload(0)
        ld("pool", onesb[:], cst_d[:, 256:384], B("onesb"))
        ld("pool", wview(W[0], 512), wsrc(w_in, C_V, 512), B("W0"), r=[B("xbuf0"), B("xbuf0b")])
        ld("pool", wf[:, :].rearrange("p (kc n) -> p kc n", n=8), wsrc(w_in, C_F, 8), B("wf"))
        ld("sp", trif[:], cst_d[:, 128:256], B("trif"))
        ld("sp", onesf[:], cst_d[:, 256:384], B("onesf"))
        for b4 in range(4):
            ld("sp", bfb[:, b4 * 8:(b4 + 1) * 8], bf_d[0:1, :].broadcast_to([128, 8]), B("bfb%d" % b4))
        xload(1, r=[B("xbuf0"), B("xbuf0b")])
        ld("sp", lng[:], lnv_d[0:1, :].broadcast_to([128, 512]), B("lng"))
        ld("sp", lnbT[:], lnbT_d[:, :], B("lnbT"))
        for g in range(8):
            pr, gp = g // 2, g % 2
            ld("sp", bsT[gp * 64:(gp + 1) * 64, pr * 128:(pr + 1) * 128], bsgu_d[g:g + 1, :].broadcast_to([64, 128]), B("bsT%d" % g))
        ld("pool", identb[:], cst_d[:, 0:128], B("identb"))
        ld("pool", trimaskb[:], cst_d[:, 384:512], B("trimaskb"))
        ld("pool", wsTb[:], wsT_d[:, :], B("wsTb"))
        bfbR = [B("bfb%d" % i) for i in range(4)]
        bsTR = [B("bsT%d" % i) for i in range(8)]
        S.op("dve", lambda e: e.memset(AK[:], 1.0), writes=[B("AK")])
        S.op("dve", lambda e: e.memset(AQ[:], 8.0), writes=[B("AQ")])
        S.op("dve", lambda e: e.memset(acc[:], 0.0), writes=[B("acc")])
        AK4 = AK[:, :].rearrange("p (b h c) -> p b h c", h=8, c=7)
        AQ4 = AQ[:, :].rearrange("p (b h c) -> p b h c", h=8, c=7)
        for h in range(8):
            cp("dve", AK4[:, :, h, 6], kmask[:, :], [B("kmask"), B("AK")], [B("AK")])

        g1 = lambda kc: gains[:, kc:kc + 1]
        g2 = lambda kc: gains[:, 8 + kc:9 + kc]
        gf = lambda kc: gains[:, 16 + kc:17 + kc]

        def rms_chunk(xsrc_fn, sqt, sqB, gfn, dst_fn, xR, dstW, psb):
            S.op("act", lambda e, a=(sqt, xsrc_fn): e.activation(out=a[0][:, :], in_=a[1](None), func=AF.Square),
                 reads=xR, writes=[sqB])
            for kc in range(8):
                mm(ps(psb), onesb[:], sqt[:, kc * 512:(kc + 1) * 512], kc == 0, kc == 7, [B("sq%d" % kc), B("onesb")], [PB[psb]])
            ts("dve", tvar[:], ps(psb), 1.0 / D, EPS, ALU.mult, ALU.add, [PB[psb]], [B("tvar")])
            act(tvar[:], tvar[:], AF.Ln, [B("tvar")], [B("tvar")])
            act(rstd[:], tvar[:], AF.Exp, [B("tvar")], [B("rstd")], scale=-0.5)
            for kc in range(8):
                S.op("dve", lambda e, a=(dst_fn(kc), xsrc_fn(kc), gfn(kc)): e.scalar_tensor_tensor(
                    out=a[0], in0=a[1], scalar=a[2], in1=rstd[:], op0=ALU.mult, op1=ALU.mult),
                    reads=xR + [B("rstd"), B("gains")], writes=dstW)

        VP4 = VP[:, :].rearrange("p (b q c) -> p b q c", q=4, c=129)
        S.op("dve", lambda e: e.memset(VP4[:, :, :, 64], 1.0), writes=[B("VP")])

        def vslab(blk, h):
            base = blk * 516 + (h // 2) * 129
            return VP[:, base:base + 65] if h % 2 == 0 else VP[:, base + 1:base + 129]
        def rms_rest(xsrc_fn, sqt, sqB, gfn, dst_fn, xR, dstW, psb):
            for kc in range(8):
                mm(ps(psb), onesb[:], sqt[:, kc * 512:(kc + 1) * 512], kc == 0, kc == 7, [B("sq%d" % kc), B("onesb")], [PB[psb]])
            ts("dve", tvar[:], ps(psb), 1.0 / D, EPS, ALU.mult, ALU.add, [PB[psb]], [B("tvar")])
            act(tvar[:], tvar[:], AF.Ln, [B("tvar")], [B("tvar")])
            act(rstd[:], tvar[:], AF.Exp, [B("tvar")], [B("rstd")], scale=-0.5)
            for kc in range(8):
                S.op("dve", lambda e, a=(dst_fn(kc), xsrc_fn(kc), gfn(kc)): e.scalar_tensor_tensor(
                    out=a[0], in0=a[1], scalar=a[2], in1=rstd[:], op0=ALU.mult, op1=ALU.mult),
                    reads=xR + [B("rstd"), B("gains")], writes=dstW)

        def stageA1sq(ci):
            if ci >= 2:
                xload(ci)
            xb = xbuf[ci % 2]
            pieces = []
            for kc in range(8):
                def piece(kc=kc, xb=xb, ci=ci):
                    S.op("act", lambda e, a=(sq[:, kc * 512:(kc + 1) * 512], xb[:, kc * 512:(kc + 1) * 512]): e.activation(out=a[0], in_=a[1], func=AF.Square),
                         reads=[B("xbuf%d%s" % (ci % 2, "" if kc < 4 else "b"))], writes=[B("sq%d" % kc)])
                pieces.append(piece)
            return pieces

        def stageA1rest(ci):
            own = ci >= 4
            t0 = (ci % 4) * 512
            xb = xbuf[ci % 2]
            xB = B("xbuf%d" % (ci % 2))
            hdst = hown if own else hprev
            hB = B("hown%d" % (ci % 4)) if own else B("hprev%d" % (ci % 4))

            def xs(kc, xb=xb):
                return xb[:, :] if kc is None else xb[:, kc * 512:(kc + 1) * 512]

            rms_rest(xs, sq, B("sq"), g1, lambda kc, hd=hdst, t0=t0: hd[:, kc * T + t0:kc * T + t0 + 512], [xB, B("xbuf%db" % (ci % 2))], [hB], 0)

        def stageA2(ci, pieces=()):
            pieces = list(pieces)
            own = ci >= 4
            t0 = (ci % 4) * 512
            hdst = hown if own else hprev
            hB = B("hown%d" % (ci % 4)) if own else B("hprev%d" % (ci % 4))
            for blk in range(4):
                gb = ci * 4 + blk
                tok = t0 + blk * 128
                pv = 1 + blk % 2
                for kc in range(8):
                    mm(ps(pv), hdst[:, kc * T + tok:kc * T + tok + 128], W[0][:, kc * 512:(kc + 1) * 512], kc == 0, kc == 7,
                       [hB, B("W0")], [PB[pv]])
                pv4 = ps(pv).rearrange("p (q e d) -> p q e d", e=2, d=64)
                S.op("act", lambda e, a=(VP4[:, gb, :, 0:64], pv4[:, :, 0, :]): e.copy(a[0], a[1]), reads=[PB[pv]], writes=[B("VP")])
                S.op("act", lambda e, a=(VP4[:, gb, :, 65:129], pv4[:, :, 1, :]): e.copy(a[0], a[1]), reads=[PB[pv]], writes=[B("VP")])
                for _ in range((0, 2, 3, 3)[blk]):
                    if pieces:
                        pieces.pop(0)()
                for kc in range(8):
                    mm(pss[:, 3 * 512 + blk * 8:3 * 512 + blk * 8 + 8], hdst[:, kc * T + tok:kc * T + tok + 128], wf[:, kc * 8:(kc + 1) * 8],
                       kc == 0, kc == 7, [hB, B("wf")], [PB[3]])
            tt("dve", zt[:], pss[:, 3 * 512:3 * 512 + 32], bfb[:], ALU.add, [PB[3]] + bfbR, [B("zt")])
            act(et[:], zt[:], AF.Exp, [B("zt")], [B("et")], scale=-1.0)
            act(spt_all[:, ci * 32:(ci + 1) * 32], et[:], AF.Ln, [B("et")], [B("spt_all%d" % ci)], bias=1.0)

        def stageA2b(ci):
            for b4 in range(4):
                o = pss[:, 4 * 512 + b4 * 8:4 * 512 + b4 * 8 + 8]
                mm(o, trif[:], spt_all[:, ci * 32 + b4 * 8:ci * 32 + (b4 + 1) * 8], True, True, [B("trif"), B("spt_all%d" % ci)], [PB[4]])
            cp("dve", w_all[:, ci * 32:(ci + 1) * 32], pss[:, 4 * 512:4 * 512 + 32], [PB[4]], [B("w_all%d" % ci)])

        def finalize_gates():
            SA = [B("spt_all%d" % i) for i in range(8)]
            WA = [B("w_all%d" % i) for i in range(8)]
            sp3 = spt_all[:, :].rearrange("p (b h) -> p b h", h=8)
            for h in range(8):
                mm(pss[0:32, 4 * 512 + h:4 * 512 + h + 1], sp3[:, :, h], onesf[:, 0:1], True, True, SA + [B("onesf")], [PB[4]])
            cp("dve", totT[0:32, :], pss[0:32, 4 * 512:4 * 512 + 8], [PB[4]], [B("totT")])
            for h in range(8):
                ts("dve", xfin[0:32, h * 32:(h + 1) * 32], bexp[0:32, h * 32:(h + 1) * 32], totT[0:32, h:h + 1], None, ALU.mult, ALU.bypass,
                   [B("bexp"), B("totT")], [B("xfin")])
            mm(pss[:, 4 * 512:4 * 512 + 256], onesf[0:32, :], xfin[0:32, :], True, True, [B("onesf"), B("xfin")], [PB[4]])
            tt("dve", cs_all[:, :].rearrange("p (b h) -> p b h", h=8), w_all[:, :].rearrange("p (b h) -> p b h", h=8),
               pss[:, 4 * 512:4 * 512 + 256].rearrange("p (h b) -> p b h", b=32), ALU.add, WA + [PB[4]], [B("cs_all")])
            cp("dve", hia[:], cs_all[:], [B("cs_all")], [B("hia")])
            tt("dve", r1a[:], cs_all[:], hia[:], ALU.subtract, [B("cs_all"), B("hia")], [B("r1a")])
            cp("dve", loa[:], r1a[:], [B("r1a")], [B("loa")])
            tt("dve", cs_all[:], r1a[:], loa[:], ALU.subtract, [B("r1a"), B("loa")], [B("cs_all")])
            cp("dve", lo2a[:], cs_all[:], [B("cs_all")], [B("lo2a")])
            for j, (tb, tn) in enumerate(((hia, "hia"), (loa, "loa"), (lo2a, "lo2a"))):
                v = tb[:, :].rearrange("p (b h) -> p b h", h=8)
                cp("dve", AK4[:, :, :, 3 + j], v, [B(tn), B("AK")], [B("AK")])
                ts("dve", AQ4[:, :, :, j], v[:, 16:32, :], -8.0, None, ALU.mult, ALU.bypass, [B(tn), B("AQ")], [B("AQ")])

        for pc in stageA1sq(0):
            pc()
        stageA1rest(0)
        for pc in stageA1sq(1):
            pc()
        for ci in range(8):
            if ci + 1 < 8:
                stageA1rest(ci + 1)
            stageA2(ci, stageA1sq(ci + 2) if ci + 2 < 8 else ())
            if ci == 1:
                ld("pool", wview(W[1], 512), wsrc(w_in, C_K, 512), B("W1"))
            if ci >= 1:
                stageA2b(ci - 1)
        stageA2b(7)
        ld("sp", bexp[0:32, :], bexp_d[:, :], B("bexp"))
        finalize_gates()

        S.op("dve", lambda e: e.memset(wsTb[64:128, :].rearrange("p (g i) -> p g i", i=128)[:, :, 0:64], 0.0),
             reads=[B("wsTb")], writes=[B("wsTb")])
        dumps = []

        def dump(name, ap_, rbufs):
            if name in dbg_d:
                dumps.append((name, ap_, rbufs))

        dump("hown", hown[:, 0:512], [B("hown0")])
        dump("AK", AK[:, :], [B("AK")])
        dump("AQ", AQ[:, :], [B("AQ")])
        dump("VP", VP[:, 16 * 516:17 * 516], [B("VP")])

        if stop != "A":
            S.reuse(B("yaT"), [B("xbuf0"), B("xbuf1"), B("xbuf0b"), B("xbuf1b")])
            for i in range(2):
                S.reuse(B("KpT%d" % i), [B("sq%d" % j) for j in range(8)])
            S.reuse(B("QpT0"), [B("sq%d" % j) for j in range(8)])
            for i in range(3):
                S.reuse(B("PT%d" % i), [B(n_) for n_ in ("spt_all%d" % i for i in range(8))] + [B("w_all%d" % i) for i in range(8)] + [B(n_) for n_ in ("cs_all", "r1a", "hia", "loa", "lo2a", "xfin", "totT", "bexp")])
            S.reuse(B("QpT1"), [B(n_) for n_ in ("spt_all%d" % i for i in range(8))] + [B("w_all%d" % i) for i in range(8)] + [B(n_) for n_ in ("cs_all", "r1a", "hia", "loa", "lo2a", "xfin", "totT", "bexp")])
            S.reuse(B("QpT1"), [B("sq%d" % j) for j in range(8)])
            ld("pool", wview(W[0], 512), wsrc(w_in, C_Q, 512), B("W0"))
            S.reuse(B("maskEb"), [B("tvar")])
            S.reuse(B("maskOb"), [B("tvar")])
            ld("pool", maskEb[:], cst_d[:, 512:640], B("maskEb"))
            ld("pool", maskOb[:], cst_d[:, 640:768], B("maskOb"))
            S.reuse(B("negb"), [B("tvar")])
            ld("pool", negb[:], cst_d[:, 768:896], B("negb"))

            def proj_units(h, banks=(7,), split=False):
                units = []
                kb = h % 2
                for ci in range(8):
                    def um(ci=ci, bk=banks[ci % len(banks)]):
                        hsrc, hB = (hprev, B("hprev%d" % ci)) if ci < 4 else (hown, B("hown%d" % (ci - 4)))
                        t0 = (ci % 4) * 512
                        for kc in range(8):
                            mm(pss[0:64, bk * 512:(bk + 1) * 512], W[1][:, kc * 512 + h * 64:kc * 512 + h * 64 + 64],
                               hsrc[:, kc * T + t0:kc * T + t0 + 512], kc == 0, kc == 7, [B("W1"), hB], [PB[bk]])

                    def ur(ci=ci, bk=banks[ci % len(banks)]):
                        for b4 in range(4):
                            mm(pss[64:71, bk * 512 + b4 * 128:bk * 512 + (b4 + 1) * 128], AK4[:, ci * 4 + b4, h, :], identb[:], True, True,
                               [B("AK"), B("identb")], [PB[bk]])
                        cp("dve", KpT[kb][0:71, ci * 512:(ci + 1) * 512], pss[0:71, bk * 512:(bk + 1) * 512], [PB[bk]], [B("KpT%d" % kb)])
                    units.append((um, ur) if split else (lambda um=um, ur=ur: (um(), ur())))
                for ci in range(4):
                    def qm(ci=ci, bk=banks[(ci + 2) % len(banks)]):
                        t0 = ci * 512
                        for kc in range(8):
                            mm(pss[0:64, bk * 512:(bk + 1) * 512], W[0][:, kc * 512 + h * 64:kc * 512 + h * 64 + 64],
                               hown[:, kc * T + t0:kc * T + t0 + 512], kc == 0, kc == 7, [B("W0"), B("hown%d" % ci)], [PB[bk]])

                    def qr(ci=ci, bk=banks[(ci + 2) % len(banks)]):
                        t0 = ci * 512
                        for b4 in range(4):
                            mm(pss[64:71, bk * 512 + b4 * 128:bk * 512 + (b4 + 1) * 128], AQ4[:, ci * 4 + b4, h, :], identb[:], True, True,
                               [B("AQ"), B("identb")], [PB[bk]])
                        cp("dve", QpT[kb][0:71, t0:t0 + 512], pss[0:71, bk * 512:(bk + 1) * 512], [PB[bk]], [B("QpT%d" % kb)])
                    units.append((qm, qr) if split else (lambda qm=qm, qr=qr: (qm(), qr())))
                return units

            up0 = proj_units(0, banks=(0, 1, 2, 3, 4, 5), split=True)
            for g0 in (0, 6):
                for um_, _ in up0[g0:g0 + 6]:
                    um_()
                for _, ur_ in up0[g0:g0 + 6]:
                    ur_()
            yk = [0]
            groups = []
            for h in range(NH):
                for I in range(4):
                    full = list(range(0, 4 * I)) + list(range(16, 16 + 4 * I))
                    glist = [([(full[i], 0), (full[i + 1], 0)], None) for i in range(0, len(full), 2)]
                    for d in (0, 2):
                        glist.append(([(4 * I + d, d * 128), (4 * I + d + 1, (d + 1) * 128)], "EO"))
                    for d in (0, 2):
                        glist.append(([(16 + 4 * I + d, d * 128), (16 + 4 * I + d + 1, (d + 1) * 128)], "TT"))
                    for gi_, (blks, diag) in enumerate(glist):
                        groups.append((h, I, blks, diag, gi_ == 0, gi_ == len(glist) - 1))
            G = len(groups)
            units = {h: proj_units(h) for h in range(1, NH)}
            upos = {h: 0 for h in range(1, NH)}
            per_head = G // NH
            every = max(1, per_head // 13)

            def flush_units(h):
                if h in units:
                    while upos[h] < len(units[h]):
                        units[h][upos[h]]()
                        upos[h] += 1

            def emit_S(k):
                h, I, blks, diag, first, last = groups[k]
                flush_units(h)
                kb = h % 2
                Kt, Qt = KpT[kb], QpT[kb]
                sb = 2 * (k % 3)
                for i, (j, c0) in enumerate(blks):
                    cs0 = blks[0][1] if diag else c0
                    o = pss[:, (sb + i) * 512 + cs0:(sb + i + 1) * 512]
                    r0, r1 = 0, 71
                    mm(o, Kt[r0:r1, j * 128:(j + 1) * 128], Qt[r0:r1, I * 512 + cs0:(I + 1) * 512], True, not diag,
                       [B("KpT%d" % kb), B("QpT%d" % kb)], [PB[sb + i]])
                    if diag:
                        mt, mB = {"T": (trimaskb, "trimaskb"), "E": (maskEb, "maskEb"), "O": (maskOb, "maskOb")}[diag[i]]
                        if cs0 < c0:
                            mm(pss[:, (sb + i) * 512 + cs0:(sb + i) * 512 + c0], identb[:], negb[:], False, False,
                               [B("identb"), B("negb")], [PB[sb + i]])
                        mm(pss[:, (sb + i) * 512 + c0:(sb + i) * 512 + c0 + 128], identb[:], mt[:], False, True,
                           [B("identb"), B(mB)], [PB[sb + i]])

            def emit_E(k):
                h, I, blks, diag, first, last = groups[k]
                sb = 2 * (k % 3)
                n = len(blks)
                c0 = blks[0][1]
                if c0 == 0:
                    o_ap, i_ap = PT[k % 3][:, 0:n * 512], pss[:, sb * 512:(sb + n) * 512]
                else:
                    o_ap = PT[k % 3][:, 0:n * 512].rearrange("p (b c) -> p b c", c=512)[:, :, c0:512]
                    i_ap = pss[:, sb * 512:(sb + n) * 512].rearrange("p (b c) -> p b c", c=512)[:, :, c0:512]
                act(o_ap, i_ap, AF.Exp, [PB[sb + i] for i in range(n)], [B("PT%d" % (k % 3))], scale=0.125)

            pending = []

            def emit_PV(k):
                h, I, blks, diag, first, last = groups[k]
                for i, (j, c0) in enumerate(blks):
                    mrow = 65 if h % 2 == 0 else 128
                    mm(pss[0:mrow, 6 * 512 + c0:7 * 512], vslab(j, h), PT[k % 3][:, i * 512 + c0:(i + 1) * 512],
                       first and i == 0, last and i == len(blks) - 1, [B("VP"), B("PT%d" % (k % 3))], [PB[6]])
                if last:
                    yi = yk[0] % 2
                    y = ysb[yi]
                    yB = B("ysb%d" % yi)
                    yk[0] += 1
                    odd = h % 2
                    lo = 64 * odd
                    drow = 63 if odd else 64
                    if odd:
                        cp("dve", y[0:128, :], pss[0:128, 6 * 512:7 * 512], [PB[6]], [yB])
                    else:
                        cp("dve", y[0:65, :], pss[0:65, 6 * 512:7 * 512], [PB[6]], [yB])
                    r_ = yk[0] - 1
                    S.dma("sp", lambda e, a=(scr_d[r_:r_ + 1, :], y[drow:drow + 1, :]): e.dma_start(out=a[0], in_=a[1]),
                          reads=[yB], writes=[B("scr%d" % yi)], sbuf=B("scr%d" % yi))
                    S.dma("sp", lambda e, a=(rec[yi][lo:lo + 64, :], scr_d[r_:r_ + 1, :].broadcast_to([64, 512])): e.dma_start(out=a[0], in_=a[1]),
                          reads=[B("scr%d" % yi)], writes=[B("bc%d" % yi)], sbuf=B("bc%d" % yi))

                    def normB(h=h, I=I, y=y, yB=yB, yi=yi, lo=lo):
                        S.op("dve", lambda e, a=rec[yi][lo:lo + 64, :]: e.reciprocal(a, a), reads=[B("bc%d" % yi)], writes=[B("bc%d" % yi)])
                        tt("dve", yaT[lo:lo + 64, (h // 2) * T + I * 512:(h // 2) * T + (I + 1) * 512], y[lo:lo + 64, :], rec[yi][lo:lo + 64, :], ALU.mult,
                           [yB, B("bc%d" % yi)], [B("yaT")])
                    tgt = k_cur[0] + 6
                    nb = next((b_ for b_ in boundaries if b_ > k_cur[0]), None)
                    if nb is not None and tgt <= nb < tgt + 5:
                        tgt = nb
                    pending.append([tgt, normB])

            boundaries = [k_ for k_ in range(G) if groups[k_][5]]
            k_cur = [0]
            emit_S(0)
            emit_S(1)
            for k in range(G):
                emit_E(k)
                if k + 2 < G:
                    emit_S(k + 2)
                k_cur[0] = k
                emit_PV(k)
                due = [p for p in pending if p[0] <= k]
                for p in due:
                    pending.remove(p)
                    p[1]()
                h = groups[k][0]
                if h + 1 < NH and (k % per_head) % every == every - 1 and upos[h + 1] < len(units[h + 1]):
                    units[h + 1][upos[h + 1]]()
                    upos[h + 1] += 1
            while pending:
                pending.pop(0)[1]()
            dump("yaT", yaT[:, 0:2048], [B("yaT")])

        if stop not in ("A", "C"):
            for nme in ("vln0", "vln1", "vtmp0", "vtmp1", "stmp0", "stmp1"):
                S.reuse(B(nme), [B("QpT0"), B("QpT1"), B("PT0"), B("PT1"), B("PT2")])
            for i in range(16):
                S.reuse(B("g32_%d" % i), [B("VP")])
            ld("pool", wview(W[0], 512), wsrc(w_in, C_U, 512), B("W0"))
            ld("pool", wview(W[1], 512), wsrc(w_in, C_SV, 512), B("W1"))
            def u_unit(n):
                I, cc = divmod(n, 4)
                pb = n % 2
                for kc in range(8):
                    mm(ps(pb), W[0][:, kc * 512 + cc * 128:kc * 512 + (cc + 1) * 128], hown[:, kc * T + I * 512:kc * T + (I + 1) * 512],
                       kc == 0, kc == 7, [B("W0"), B("hown%d" % I)], [PB[pb]])
                act(uT[:, cc * T + I * 512:cc * T + (I + 1) * 512], ps(pb), AF.Gelu, [PB[pb]], [B("uT%d" % I)])

            for i in range(4):
                S.reuse(B("uT%d" % i), [B("KpT0"), B("KpT1")])
            for blk in range(16):
                pb = 4 + blk % 4
                for kc in range(8):
                    mm(ps(pb), hown[:, kc * T + blk * 128:kc * T + (blk + 1) * 128], W[1][:, kc * 512:(kc + 1) * 512], kc == 0, kc == 7,
                       [B("hown%d" % (blk // 4)), B("W1")], [PB[pb]])
                gsl = g32all[:, blk * 512:(blk + 1) * 512]
                act(gsl, ps(pb), AF.Gelu, [PB[pb]], [B("g32_%d" % blk)])
                S.op("dve", lambda e, a=(st6[:, 0:6], gsl): e.bn_stats(a[0], a[1]), reads=[B("g32_%d" % blk)], writes=[B("st6")])
                S.op("dve", lambda e, a=(mv16[:, blk * 2:blk * 2 + 2], st6[:, 0:6]): e.bn_aggr(a[0], a[1]), reads=[B("st6")], writes=[B("mv16")])
            for n in range(4):
                u_unit(n)
            mvv = mv16[:, :].rearrange("p (b c) -> p b c", c=2)
            ts("dve", tv16[:], mvv[:, :, 1], EPS, None, ALU.add, ALU.bypass, [B("mv16")], [B("tv16")])
            act(tv16[:], tv16[:], AF.Ln, [B("tv16")], [B("tv16")])
            act(rs16[:], tv16[:], AF.Exp, [B("tv16")], [B("rs16")], scale=-0.5)
            S.op("dve", lambda e: e.scalar_tensor_tensor(out=nmr16[:], in0=mvv[:, :, 0], scalar=-1.0, in1=rs16[:], op0=ALU.mult, op1=ALU.mult),
                 reads=[B("mv16"), B("rs16")], writes=[B("nmr16")])
            for g in range(8):
                pr, gp = g // 2, g % 2
                mm(pss[gp * 64:(gp + 1) * 64, 3 * 512 + pr * 128:3 * 512 + (pr + 1) * 128], onesb[:, 0:64], wsTb[:, g * 128:(g + 1) * 128], True, True,
                   [B("onesb"), B("wsTb")], [PB[3]])
            for pr in range(4):
                S.op("dve", lambda e, a=(bsT[:, pr * 128:(pr + 1) * 128], pss[:, 3 * 512 + pr * 128:3 * 512 + (pr + 1) * 128], lnbT[:, pr:pr + 1]): e.scalar_tensor_tensor(
                    out=a[0], in0=a[1], scalar=a[2], in1=a[0], op0=ALU.mult, op1=ALU.add), reads=[PB[3], B("lnbT")] + bsTR, writes=[B("bias2")])
            def d_norm(blk):
                i2 = blk % 2
                gsl = g32all[:, blk * 512:(blk + 1) * 512]
                S.op("act", lambda e, a=(vtmp[i2][:], gsl, rs16[:, blk:blk + 1], nmr16[:, blk:blk + 1]): e.activation(
                    out=a[0], in_=a[1], func=AF.Identity, scale=a[2], bias=a[3]),
                    reads=[B("g32_%d" % blk), B("rs16"), B("nmr16")], writes=[B("vtmp%d" % i2)])
                tt("dve", vln[i2][:], vtmp[i2][:], lng[:], ALU.mult, [B("vtmp%d" % i2), B("lng")], [B("vln%d" % i2)])

            d_norm(0)
            for blk in range(16):
                i2 = blk % 2
                if blk + 4 < 16:
                    u_unit(blk + 4)
                pb = 2 + blk % 2
                for g in range(8):
                    pr, gp = g // 2, g % 2
                    mm(pss[gp * 64:(gp + 1) * 64, pb * 512 + pr * 128:pb * 512 + (pr + 1) * 128], vln[i2][:, g * 64:(g + 1) * 64],
                       wsTb[:, g * 128:(g + 1) * 128], True, True, [B("vln%d" % i2), B("wsTb")], [PB[pb]])
                if blk + 1 < 16:
                    d_norm(blk + 1)
                tt("dve", stmp[i2][:], ps(pb), bsT[:], ALU.add, [PB[pb], B("bias2")], [B("stmp%d" % i2)])
                uv = uT[:, :].rearrange("p (c t) -> p c t", t=T)[:, :, blk * 128:(blk + 1) * 128]
                tt("dve", uv, stmp[i2][:, :].rearrange("p (c t) -> p c t", t=128), uv, ALU.mult, [B("stmp%d" % i2), B("uT%d" % (blk // 4))], [B("uT%d" % (blk // 4))])
            dump("sT", uT[:, 0:2048], [B("uT0")])

        if stop not in ("A", "C", "D"):
            S.reuse(B("mergedT"), [B("hprev%d" % i) for i in range(4)])
            for i in range(2):
                for j in range(2):
                    S.reuse(B("sig%d%d" % (i, j)), [B("g32_%d" % i) for i in range(16)])
                    S.reuse(B("m1%d%d" % (i, j)), [B("g32_%d" % i) for i in range(16)])
            for s_ in range(2):
                for n_ in ("Wb%d", "Wbb%d", "Wbga%d", "Wbgb%d"):
                    S.reuse(B(n_ % s_), [B("W%d" % s_)])

            def bslot(dc):
                s = dc % 2
                if dc < 6:
                    return W[s], ["Wb%d" % s, "Wbb%d" % s, "Wbga%d" % s, "Wbgb%d" % s]
                return Walt[s], ["Xb%d" % s, "Xbb%d" % s, "Xbga%d" % s, "Xbgb%d" % s]

            for s_ in range(2):
                for n_ in ("Xb%d", "Xbb%d", "Xbga%d", "Xbgb%d"):
                    S.reuse(B(n_ % s_), [B("g32_%d" % i) for i in range(16)])

            def bundle(dc):
                wt, nm = bslot(dc)
                c0 = dc * 128
                ld("pool", wt[:, 0:512].rearrange("p (k n) -> p k n", n=128), wsrc(w_a, c0, 128, nk=4), B(nm[0]))
                ld("pool", wt[:, 1024:1536].rearrange("p (k n) -> p k n", n=128), wsrc(w_b, c0, 128, nk=4), B(nm[1]))
                ld("pool", wt[:, 1536:2560].rearrange("p (k n) -> p k n", n=128), wsrc(w_in, C_GA + c0, 128), B(nm[2]))
                ld("pool", wt[:, 2560:3584].rearrange("p (k n) -> p k n", n=128), wsrc(w_in, C_GB + c0, 128), B(nm[3]))

            bundle(0)
            k = 0
            for dc in range(8):
                wt, nm = bslot(dc)
                if dc + 1 < 8:
                    bundle(dc + 1)
                if dc == 5:
                    S.reuse(B("Wo0"), [B(n % 0) for n in ("Wb%d", "Wbb%d", "Wbga%d", "Wbgb%d")])
                    ld("pool", Wo[:, :].rearrange("p (kc n) -> p kc n", n=1024)[:, 0:4, :], wsrc(w_o, 0, 1024, r0=0, nk=4), B("Wo0"))
                if dc == 6:
                    S.reuse(B("Wo1"), [B(n % 1) for n in ("Wb%d", "Wbb%d", "Wbga%d", "Wbgb%d")])
                    ld("pool", Wo[:, :].rearrange("p (kc n) -> p kc n", n=1024)[:, 4:8, :], wsrc(w_o, 0, 1024, r0=512, nk=4), B("Wo1"))
                for I in range(4):
                    p0 = 4 * (k % 2)
                    k2 = k % 2
                    k += 1
                    tsl = slice(I * 512, (I + 1) * 512)
                    for kc in range(4):
                        mm(ps(p0), wt[:, kc * 128:(kc + 1) * 128], yaT[:, kc * T + I * 512:kc * T + (I + 1) * 512], kc == 0, kc == 3,
                           [B(nm[0]), B("yaT")], [PB[p0]])
                    for kc in range(4):
                        mm(ps(p0 + 1), wt[:, 1024 + kc * 128:1024 + (kc + 1) * 128], uT[:, kc * T + I * 512:kc * T + (I + 1) * 512], kc == 0, kc == 3,
                           [B(nm[1]), B("uT%d" % I)], [PB[p0 + 1]])
                    for kc in range(8):
                        mm(ps(p0 + 2), wt[:, 1536 + kc * 128:1536 + (kc + 1) * 128], hown[:, kc * T + I * 512:kc * T + (I + 1) * 512], kc == 0, kc == 7,
                           [B(nm[2]), B("hown%d" % I)], [PB[p0 + 2]])
                    for kc in range(8):
                        mm(ps(p0 + 3), wt[:, 2560 + kc * 128:2560 + (kc + 1) * 128], hown[:, kc * T + I * 512:kc * T + (I + 1) * 512], kc == 0, kc == 7,
                           [B(nm[3]), B("hown%d" % I)], [PB[p0 + 3]])
                    act(sig[k2][0][:], ps(p0 + 2), AF.Sigmoid, [PB[p0 + 2]], [B("sig%d0" % k2)])
                    act(sig[k2][1][:], ps(p0 + 3), AF.Sigmoid, [PB[p0 + 3]], [B("sig%d1" % k2)])
                    tt("dve", m1[k2][0][:], sig[k2][0][:], ps(p0), ALU.mult, [B("sig%d0" % k2), PB[p0]], [B("m1%d0" % k2)])
                    tt("dve", m1[k2][1][:], sig[k2][1][:], ps(p0 + 1), ALU.mult, [B("sig%d1" % k2), PB[p0 + 1]], [B("m1%d1" % k2)])
                    tt("pool", mergedT[:, dc * T + I * 512:dc * T + (I + 1) * 512], m1[k2][0][:], m1[k2][1][:], ALU.add,
                       [B("m1%d0" % k2), B("m1%d1" % k2)], [B("mergedT")])
            dump("mergedT", mergedT[:, 0:2048], [B("mergedT")])

            for dc_ in range(8):
                for i_ in range(4):
                    S.reuse(B("x1T%d_%d" % (dc_, i_)), [B("hown%d" % i) for i in range(4)] + [B("yaT")])
            for i in range(3):
                S.reuse(B("xres%d" % i), [B(n_) for n_ in ("vln0", "vln1", "vtmp0", "vtmp1", "stmp0", "stmp1", "PT0", "PT1", "PT2", "QpT0", "QpT1")])
            for kc_ in range(8):
                S.reuse(B("sq2_%d" % kc_), [B(n_) for n_ in ("vln0", "vln1", "vtmp0", "vtmp1", "stmp0", "stmp1", "PT0", "PT1", "PT2", "QpT0", "QpT1")])
            Wo3 = Wo[:, :].rearrange("p (kc n) -> p kc n", n=1024)
            xres7 = xres + [ysb[0], ysb[1], rec[0], rec[1]]
            for i in range(3, 7):
                S.reuse(B("xres%d" % i), [B("ysb0"), B("ysb1"), B("rec0"), B("rec1"), B("bc0"), B("bc1")])
            S.reuse(B("tvar"), [B("maskEb"), B("maskOb"), B("negb")])
            for i_ in range(4):
                S.reuse(B("h2T%d" % i_), [B(n_ % s_) for s_ in range(2) for n_ in ("Xb%d", "Xbb%d", "Xbga%d", "Xbgb%d")] + [B("m1%d%d" % (a, b)) for a in range(2) for b in range(2)] + [B("sig%d%d" % (a, b)) for a in range(2) for b in range(2)] + [B("g32_%d" % i) for i in range(16)])
            for i in range(2):
                S.reuse(B("aT%d" % i), [B("mergedT")])
                S.reuse(B("rl%d" % i), [B("mergedT")])
                S.reuse(B("W%d" % (2 + i)), [B("uT%d" % j) for j in range(4)] + [B("vln0"), B("vln1")])
            WU = [W[0], W23[0]]
            WD = [W[1], W23[1]]
            WUB = [B("W0"), B("W2")]
            WDB = [B("W1"), B("W3")]

            def ffn_load(e8):
                s = (e8 + 1) % 2
                ld("pool", wview(WU[s], 512), wsrc(w_up, e8 * 512, 512), WUB[s])
                ld("pool", WD[s][:, :].rearrange("p (k n) -> p k n", n=1024),
                   w_down[e8 * 512:(e8 + 1) * 512, :].rearrange("(k p) n -> p k n", p=128), WDB[s])

            def xsF(I):
                def xs(kc):
                    if kc is None:
                        return x1T[:, :].rearrange("p (kc t) -> p kc t", t=T)[:, :, I * 512:(I + 1) * 512]
                    return x1T[:, kc * T + I * 512:kc * T + (I + 1) * 512]
                return xs

            def normA(I):
                xs = xsF(I)
                for kc in range(8):
                    S.op("act", lambda e, a=(sq2[:, kc * 512:(kc + 1) * 512], xs(kc)): e.activation(out=a[0], in_=a[1], func=AF.Square),
                         reads=[B("x1T%d_%d" % (kc, I))], writes=[B("sq2_%d" % kc)])

            def normB_pieces(I, gfn, dst_fn, dstB_fn, post=None):
                xs = xsF(I)

                def pre():
                    for kc in range(8):
                        mm(ps(7), onesb[:], sq2[:, kc * 512:(kc + 1) * 512], kc == 0, kc == 7, [B("sq2_%d" % kc), B("onesb")], [PB[7]])
                    ts("dve", tvar[:], ps(7), 1.0 / D, EPS, ALU.mult, ALU.add, [PB[7]], [B("tvar")])
                    act(tvar[:], tvar[:], AF.Ln, [B("tvar")], [B("tvar")])
                    act(rstd[:], tvar[:], AF.Exp, [B("tvar")], [B("rstd")], scale=-0.5)

                def mk(kc):
                    def piece():
                        S.op("dve", lambda e, a=(dst_fn(kc), xs(kc), gfn(kc)): e.scalar_tensor_tensor(
                            out=a[0], in0=a[1], scalar=a[2], in1=rstd[:], op0=ALU.mult, op1=ALU.mult),
                            reads=[B("x1T%d_%d" % (kc, I)), B("rstd"), B("gains")], writes=[dstB_fn(kc)])
                        if post is not None:
                            post(kc)
                    return piece
                return [pre] + [mk(kc) for kc in range(8)]

            def normB(I, gfn, dst_fn, dstB_fn, post=None):
                for pc in normB_pieces(I, gfn, dst_fn, dstB_fn, post):
                    pc()

            def rms2B(I):
                normB(I, g2, lambda kc: h2T[:, kc * T + I * 512:kc * T + (I + 1) * 512], lambda kc: B("h2T%d" % I))

            ffn_load(0)
            k = 0
            for I in range(4):
                pcs = normB_pieces(I - 1, g2, lambda kc, I_=I - 1: h2T[:, kc * T + I_ * 512:kc * T + (I_ + 1) * 512], lambda kc, I_=I - 1: B("h2T%d" % I_)) if I >= 1 else []
                for dc in range(8):
                    pb = k % 7
                    xr = xres7[k % 7]
                    xrB = B("xres%d" % (k % 7))
                    k += 1
                    ld("sp", xr[:], xown[dc * 128:(dc + 1) * 128, I * 512:(I + 1) * 512], xrB)
                    for kc in range(8):
                        mm(ps(pb), Wo[:, kc * 1024 + dc * 128:kc * 1024 + (dc + 1) * 128], mergedT[:, kc * T + I * 512:kc * T + (I + 1) * 512],
                           kc == 0, kc == 7, [B("Wo%d" % (kc // 4)), B("mergedT")], [PB[pb]])
                    tt("dve", x1T[:, dc * T + I * 512:dc * T + (I + 1) * 512], ps(pb), xr[:], ALU.add, [PB[pb], xrB], [B("x1T%d_%d" % (dc, I))])
                    if pcs:
                        pcs.pop(0)()
                    if dc == 7:
                        while pcs:
                            pcs.pop(0)()
                        normA(I)
            dump("x1T", x1T[:, 0:2048], [B("x1T0_0")])

        if stop not in ("A", "C", "D", "E"):
            NF = 32 if stop != "F1" else 0

            def ffn_up(n):
                e8, I = divmod(n, 4)
                s = (e8 + 1) % 2
                a2 = n % 2
                for fc in range(4):
                    for kc in range(8):
                        mm(ps(fc), WU[s][:, kc * 512 + fc * 128:kc * 512 + (fc + 1) * 128], h2T[:, kc * T + I * 512:kc * T + (I + 1) * 512],
                           kc == 0, kc == 7, [WUB[s], B("h2T%d" % I)], [PB[fc]])
                    act(rl[a2][:, fc * 512:(fc + 1) * 512], ps(fc), AF.Relu, [PB[fc]], [B("rl%d_%d" % (a2, fc))])
                    tt("pool", aT[a2][:, fc * 512:(fc + 1) * 512], rl[a2][:, fc * 512:(fc + 1) * 512], rl[a2][:, fc * 512:(fc + 1) * 512], ALU.mult,
                       [B("rl%d_%d" % (a2, fc))], [B("aT%d_%d" % (a2, fc))])

            def ffn_down(n, mid=None):
                mid = list(mid) if mid else []
                e8, I = divmod(n, 4)
                s = (e8 + 1) % 2
                a2 = n % 2
                for dc in range(8):
                    pb = 4 + dc % 4
                    for fc in range(4):
                        mm(ps(pb), WD[s][:, fc * 1024 + dc * 128:fc * 1024 + (dc + 1) * 128], aT[a2][:, fc * 512:(fc + 1) * 512],
                           fc == 0, fc == 3, [WDB[s], B("aT%d_%d" % (a2, fc))], [PB[pb]])
                    xsl = x1T[:, dc * T + I * 512:dc * T + (I + 1) * 512]
                    tt("dve", xsl, ps(pb), xsl, ALU.add, [PB[pb], B("x1T%d_%d" % (dc, I))], [B("x1T%d_%d" % (dc, I))])
                    if mid:
                        mid.pop(0)()

            for i in range(2):
                for fc in range(4):
                    S.reuse(B("aT%d_%d" % (i, fc)), [B("mergedT")])
                    S.reuse(B("rl%d_%d" % (i, fc)), [B("mergedT")])
            ostg = [ysb[0], ysb[1], rec[0], rec[1]]
            for i in range(4):
                S.reuse(B("ostg%d" % i), [B("ysb0"), B("ysb1"), B("rec0"), B("rec1"), B("bc0"), B("bc1")] + [B("xres%d" % j) for j in range(3, 7)])
            ocnt = [0]

            def finalB_pieces(I):
                def post(kc):
                    oi = ocnt[0] % 4
                    S.dma("sp", lambda e, a=(out_d[kc * 128:(kc + 1) * 128, I * 512:(I + 1) * 512], ostg[oi][:]): e.dma_start(out=a[0], in_=a[1]),
                          reads=[B("ostg%d" % oi)], sbuf=B("ostg%d" % oi))
                    ocnt[0] += 1
                return normB_pieces(I, gf, lambda kc: ostg[(ocnt[0]) % 4][:], lambda kc: B("ostg%d" % (ocnt[0] % 4)), post=post)

            def finalB(I):
                for pc in finalB_pieces(I):
                    pc()

            S.reuse(B("W0"), [B("Wo0"), B("Wo1")])
            S.reuse(B("W1"), [B("Wo0"), B("Wo1")])
            if NF:
                ffn_load(1)
                ffn_up(0)
            rms2B(3)
            for n in range(NF):
                if n + 1 < NF:
                    ffn_up(n + 1)
                fin = stop not in ("F1", "F2") and n >= 28
                rest = finalB_pieces(n - 29) if (fin and n >= 29) else []
                ffn_down(n, mid=rest[:8])
                for pc in rest[8:]:
                    pc()
                if n % 4 == 3 and n // 4 + 2 < 8:
                    ffn_load(n // 4 + 2)
                if fin:
                    normA(n - 28)
            dump("x2T", x1T[:, 0:2048], [B("x1T0_0")])
            dump("h2T", h2T[:, 0:2048], [B("h2T0")])
            if stop not in ("F1", "F2"):
                finalB(3)
                for i in range(4):
                    S.op("sp", None, writes=[B("ostg%d" % i)])

        if dumps:
            dstage = at("dstage", [128, 4096], F32, OFF_T)
            for name, ap_, rb in dumps:
                ncol = ap_.shape[1]
                np_ = ap_.shape[0]
                dsB = B("dstage")
                S.reuse(dsB, list(Bn.values()))
                cp("dve", dstage[0:np_, 0:ncol], ap_, rb, [dsB])
                S.dma("sp", lambda e, a=(dbg_d[name], dstage[0:np_, 0:ncol]): e.dma_start(out=a[0][:, :], in_=a[1]), reads=[dsB], sbuf=dsB)
            S.op("sp", None, writes=[B("dstage")])
        if stop is not None:
            pass
        S.emit(st)
    return nc


def _consts():
    ident = np.eye(128, dtype=np.float32)
    tri = np.triu(np.ones((128, 128), np.float32))
    ones = np.ones((128, 128), np.float32)
    trimask = np.where(np.arange(128)[:, None] <= np.arange(128)[None, :], 0.0, NEG).astype(np.float32)
    return [ident, tri, ones, trimask]


def _blocks(r):
    own = sorted([4 * j + (0 if r == 0 else 1) for j in range(8)] + [4 * j + (3 if r == 0 else 2) for j in range(8)])
    oth = sorted(set(range(32)) - set(own))
    return own, oth


def make_in_maps(x, norm1_g, w_in, b_f, ln_v_g, ln_v_b, w_sgu, b_sgu, w_a, w_b, w_o, norm2_g, w_up, w_down, normf_g):
    f = lambda a: np.ascontiguousarray(np.asarray(a, dtype=np.float32))
    x = f(x)
    g3 = np.concatenate([f(norm1_g)[0].reshape(8, 128).T, f(norm2_g)[0].reshape(8, 128).T, f(normf_g).reshape(8, 128).T], axis=1)
    common = {
        "w_in": f(w_in)[0], "w_a": f(w_a)[0], "w_b": f(w_b)[0], "w_o": f(w_o)[0], "w_up": f(w_up)[0], "w_down": f(w_down)[0],
        "gains": f(g3), "lnbT": f(f(ln_v_b)[0].reshape(4, 128).T), "b_f": f(b_f).reshape(1, 8), "lnv": f(np.stack([f(ln_v_g)[0], f(ln_v_b)[0]])),
        "wsT": f(np.transpose(f(w_sgu)[0], (2, 0, 1)).reshape(128, 8 * 128)), "b_sgu": f(b_sgu)[0], "cst": _consts(),
    }
    common.pop("cst")
    maps = []
    for c in range(8):
        b, r = c // 2, c % 2
        own, oth = _blocks(r)
        gat = lambda blks: f(np.concatenate([x[b, g * 128:(g + 1) * 128, :] for g in blks], axis=0).T)
        glob = np.array(oth + own)
        before = (glob[:, None] < glob[None, :]).astype(np.float32)
        bexp = np.ascontiguousarray(np.tile(before, (1, 8)))
        full = np.full((128, 128), NEG, np.float32)
        zero = np.zeros((128, 128), np.float32)
        cst = np.ascontiguousarray(np.concatenate(_consts() + ([full, zero] if r == 0 else [zero, full]) + [full], axis=1))
        m = dict(common)
        m.update({"xprev": gat(oth), "xown": gat(own), "kmask": np.zeros((128, 32), np.float32), "cst": cst, "bexp": bexp})
        maps.append(m)
    return maps


_NC_CACHE = {}


def kernel(**inputs):
    if "nc" not in _NC_CACHE:
        _NC_CACHE["nc"] = build()
    nc = _NC_CACHE["nc"]
    maps = make_in_maps(**inputs)
    res = run_bass_kernel_spmd(nc, maps, core_ids=list(range(8)))
    out = np.empty((4, 4096, D), np.float32)
    for c in range(8):
        b, r = c // 2, c % 2
        own, _ = _blocks(r)
        o = res.results[c]["out"].T
        for l, g in enumerate(own):
            out[b, g * 128:(g + 1) * 128, :] = o[l * 128:(l + 1) * 128, :]
    return out
```
